# Optimizing a Trainium2 kernel written in Bass

```python
import jax, jax.numpy as jnp
from jax import lax
import numpy as np

D_MODEL = 1024
BATCH = 8
SEQ = 4096
DEPTH = 2

PLE_DIM = 256
D_FF = 2752
M_HEADS = 4
M_HEAD_DIM = 256
M_WIDTH = M_HEADS * M_HEAD_DIM
M_CHUNK = 64
CONV_K = 4
A_Q_HEADS = 16
A_KV_HEADS = 4
A_HEAD_DIM = 64
A_GROUP = A_Q_HEADS // A_KV_HEADS
A_WIDTH = A_Q_HEADS * A_HEAD_DIM
A_KV_WIDTH = A_KV_HEADS * A_HEAD_DIM
WINDOW = 128
A_BLOCK = 128
EPS = 1e-6
N_IN = 4 * M_WIDTH + 2 * M_HEADS + A_WIDTH + 2 * A_KV_WIDTH + 2 * D_MODEL

kernel_name = "hybrid_mlstm_swa_macaron_ple"


def rmsnorm(x, g):
    xf = x.astype(jnp.float32)
    y = xf * lax.rsqrt(jnp.mean(xf * xf, axis=-1, keepdims=True) + EPS)
    return (y * g.astype(jnp.float32)).astype(x.dtype)


def head_rmsnorm(x, g):
    xf = x.astype(jnp.float32)
    y = xf * lax.rsqrt(jnp.mean(xf * xf, axis=-1, keepdims=True) + EPS)
    return y * g.astype(jnp.float32)


def swiglu(x, w_gate, w_up, w_down):
    return (jax.nn.silu(x @ w_gate) * (x @ w_up)) @ w_down


def causal_dwconv(x, w, b):
    c = x.shape[-1]
    y = lax.conv_general_dilated(x, w[:, None, :].astype(x.dtype), window_strides=(1,),
                                 padding=[(CONV_K - 1, 0)],
                                 dimension_numbers=("NWC", "WIO", "NWC"),
                                 feature_group_count=c)
    return y + b


def mlstm_chunkwise(q, k, v, ig, lf):
    bsz, t, nh, dh = q.shape
    nc = t // M_CHUNK

    def to_chunks(z):
        return z.reshape(bsz, nc, M_CHUNK, nh, dh).transpose(1, 0, 3, 2, 4)

    def g_chunks(z):
        return z.reshape(bsz, nc, M_CHUNK, nh).transpose(1, 0, 3, 2)

    causal = jnp.tril(jnp.ones((M_CHUNK, M_CHUNK), dtype=bool))

    def step(carry, xs):
        c_st, n_st, m_st = carry
        qc, kc, vc, ic, fc = xs
        b = jnp.cumsum(fc, axis=-1)
        dlog = jnp.where(causal, b[..., :, None] - b[..., None, :] + ic[..., None, :], -jnp.inf)
        inter = b + m_st[..., None]
        m_t = jnp.maximum(inter, jnp.max(dlog, axis=-1))
        w_intra = jnp.exp(dlog - m_t[..., None])
        w_inter = jnp.exp(inter - m_t)
        s = jnp.einsum("bhtk,bhsk->bhts", qc, kc) * w_intra
        num = jnp.einsum("bhts,bhsv->bhtv", s, vc) + w_inter[..., None] * jnp.einsum("bhtk,bhvk->bhtv", qc, c_st)
        den = jnp.sum(s, axis=-1) + w_inter * jnp.einsum("bhtk,bhk->bht", qc, n_st)
        h = num / jnp.maximum(jnp.abs(den), jnp.exp(-m_t))[..., None]
        b_last = b[..., -1]
        a = b_last[..., None] - b + ic
        m_new = jnp.maximum(b_last + m_st, jnp.max(a, axis=-1))
        w_state = jnp.exp(a - m_new[..., None])
        decay = jnp.exp(b_last + m_st - m_new)
        c_new = decay[..., None, None] * c_st + jnp.einsum("bhsv,bhsk->bhvk", vc * w_state[..., None], kc)
        n_new = decay[..., None] * n_st + jnp.einsum("bhs,bhsk->bhk", w_state, kc)
        return (c_new, n_new, m_new), h

    init = (jnp.zeros((bsz, nh, dh, dh), jnp.float32),
            jnp.zeros((bsz, nh, dh), jnp.float32),
            jnp.zeros((bsz, nh), jnp.float32))
    xs = (to_chunks(q), to_chunks(k), to_chunks(v), g_chunks(ig), g_chunks(lf))
    _, hs = lax.scan(step, init, xs)
    return hs.transpose(1, 0, 3, 2, 4).reshape(bsz, t, nh, dh)


def swa_gqa_sinks(q, k, v, sinks):
    bsz, t = q.shape[0], q.shape[1]
    nb = t // A_BLOCK
    qb = q.reshape(bsz, nb, A_BLOCK, A_KV_HEADS, A_GROUP, A_HEAD_DIM)

    def windows(z):
        zp = jnp.pad(z, ((0, 0), (A_BLOCK, 0), (0, 0), (0, 0)))
        zb = zp.reshape(bsz, nb + 1, A_BLOCK, A_KV_HEADS, A_HEAD_DIM)
        return jnp.concatenate([zb[:, :-1], zb[:, 1:]], axis=2)

    kw, vw = windows(k), windows(v)
    scores = jnp.einsum("bnqhgd,bnkhd->bnhgqk", qb, kw).astype(jnp.float32) * (A_HEAD_DIM ** -0.5)
    qi = jnp.arange(A_BLOCK)[:, None]
    kj = jnp.arange(2 * A_BLOCK)[None, :]
    rel = A_BLOCK + qi - kj
    band = (rel >= 0) & (rel < WINDOW)
    key_pos = jnp.arange(nb)[:, None, None] * A_BLOCK + kj[None] - A_BLOCK
    valid = band[None] & (key_pos >= 0)
    scores = jnp.where(valid[None, :, None, None], scores, -jnp.inf)
    sink = sinks.astype(jnp.float32).reshape(A_KV_HEADS, A_GROUP)[None, None, :, :, None, None]
    mx = jnp.maximum(jnp.max(scores, axis=-1, keepdims=True), sink)
    pr = jnp.exp(scores - mx)
    den = jnp.sum(pr, axis=-1, keepdims=True) + jnp.exp(sink - mx)
    attn = (pr / den).astype(v.dtype)
    out = jnp.einsum("bnhgqk,bnkhd->bnqhgd", attn, vw)
    return out.reshape(bsz, t, A_WIDTH)


def hybrid_layer(h, p_i, ffn1_norm, ffn1_w_gate, ffn1_w_up, ffn1_w_down, mix_norm, w_in,
                 m_conv_w, m_conv_b, m_igate_b, m_fgate_b, m_out_norm, a_q_norm, a_k_norm,
                 a_sinks, w_branch_m, w_branch_a, w_out, ffn2_norm, ffn2_w_gate, ffn2_w_up,
                 ffn2_w_down, ple_norm, ple_gate_w, ple_proj_w):
    bsz, t, _ = h.shape
    h = h + 0.5 * swiglu(rmsnorm(h, ffn1_norm), ffn1_w_gate, ffn1_w_up, ffn1_w_down)

    u = rmsnorm(h, mix_norm)
    z = u @ w_in
    sizes = [2 * M_WIDTH, M_WIDTH, M_WIDTH, M_HEADS, M_HEADS, A_WIDTH, A_KV_WIDTH, A_KV_WIDTH, D_MODEL, D_MODEL]
    offs = [int(o) for o in np.cumsum(sizes)[:-1]]
    m_qk, m_v, m_o, m_i, m_f, a_q, a_k, a_v, g_m, g_a = jnp.split(z, offs, axis=-1)

    m_qk = jax.nn.silu(causal_dwconv(m_qk, m_conv_w, m_conv_b))
    m_q, m_k = jnp.split(m_qk, 2, axis=-1)
    mq = m_q.astype(jnp.float32).reshape(bsz, t, M_HEADS, M_HEAD_DIM) * (M_HEAD_DIM ** -0.5)
    mk = m_k.astype(jnp.float32).reshape(bsz, t, M_HEADS, M_HEAD_DIM)
    mv = m_v.astype(jnp.float32).reshape(bsz, t, M_HEADS, M_HEAD_DIM)
    ig = m_i.astype(jnp.float32) + m_igate_b.astype(jnp.float32)
    lf = jax.nn.log_sigmoid(m_f.astype(jnp.float32) + m_fgate_b.astype(jnp.float32))
    hm = mlstm_chunkwise(mq, mk, mv, ig, lf)
    hm = head_rmsnorm(hm, m_out_norm.reshape(M_HEADS, M_HEAD_DIM)).reshape(bsz, t, M_WIDTH)
    hm = (jax.nn.sigmoid(m_o.astype(jnp.float32)) * hm).astype(h.dtype)

    aq = head_rmsnorm(a_q.reshape(bsz, t, A_Q_HEADS, A_HEAD_DIM), a_q_norm).astype(h.dtype)
    ak = head_rmsnorm(a_k.reshape(bsz, t, A_KV_HEADS, A_HEAD_DIM), a_k_norm).astype(h.dtype)
    av = a_v.reshape(bsz, t, A_KV_HEADS, A_HEAD_DIM)
    ha = swa_gqa_sinks(aq, ak, av, a_sinks)

    merged = jax.nn.sigmoid(g_m) * (hm @ w_branch_m) + jax.nn.sigmoid(g_a) * (ha @ w_branch_a)
    h = h + merged @ w_out

    h = h + 0.5 * swiglu(rmsnorm(h, ffn2_norm), ffn2_w_gate, ffn2_w_up, ffn2_w_down)

    h = h + jax.nn.sigmoid(rmsnorm(h, ple_norm) @ ple_gate_w) * (p_i @ ple_proj_w)
    return h


def setup_inputs(seed: int = 0) -> dict:
    key = jax.random.key(seed)
    ks = jax.random.split(key, 32)

    def w(k, fan_in, fan_out, scale=1.0):
        return jax.random.normal(k, (DEPTH, fan_in, fan_out), jnp.float32) * (scale * fan_in ** -0.5)

    def gain(k, n):
        return 1.0 + 0.05 * jax.random.normal(k, (DEPTH, n), jnp.float32)

    f_bias = jnp.linspace(3.0, 6.0, M_HEADS, dtype=jnp.float32)[None, :] + 0.1 * jax.random.normal(ks[10], (DEPTH, M_HEADS), jnp.float32)
    return {
        "x": jax.random.normal(ks[0], (BATCH, SEQ, D_MODEL), jnp.float32),
        "p": jax.random.normal(ks[1], (DEPTH, BATCH, SEQ, PLE_DIM), jnp.float32),
        "ffn1_norm": gain(ks[2], D_MODEL),
        "ffn1_w_gate": w(ks[3], D_MODEL, D_FF),
        "ffn1_w_up": w(ks[4], D_MODEL, D_FF),
        "ffn1_w_down": w(ks[5], D_FF, D_MODEL, 0.5),
        "mix_norm": gain(ks[6], D_MODEL),
        "w_in": w(ks[7], D_MODEL, N_IN),
        "m_conv_w": jax.random.normal(ks[8], (DEPTH, CONV_K, 2 * M_WIDTH), jnp.float32) * (CONV_K ** -0.5),
        "m_conv_b": 0.02 * jax.random.normal(ks[9], (DEPTH, 2 * M_WIDTH), jnp.float32),
        "m_igate_b": 0.1 * jax.random.normal(ks[11], (DEPTH, M_HEADS), jnp.float32),
        "m_fgate_b": f_bias,
        "m_out_norm": gain(ks[12], M_WIDTH),
        "a_q_norm": gain(ks[13], A_HEAD_DIM),
        "a_k_norm": gain(ks[14], A_HEAD_DIM),
        "a_sinks": 0.5 * jax.random.normal(ks[15], (DEPTH, A_Q_HEADS), jnp.float32),
        "w_branch_m": w(ks[16], M_WIDTH, D_MODEL),
        "w_branch_a": w(ks[17], A_WIDTH, D_MODEL),
        "w_out": w(ks[18], D_MODEL, D_MODEL, 0.5),
        "ffn2_norm": gain(ks[19], D_MODEL),
        "ffn2_w_gate": w(ks[20], D_MODEL, D_FF),
        "ffn2_w_up": w(ks[21], D_MODEL, D_FF),
        "ffn2_w_down": w(ks[22], D_FF, D_MODEL, 0.5),
        "ple_norm": gain(ks[23], D_MODEL),
        "ple_gate_w": w(ks[24], D_MODEL, D_MODEL),
        "ple_proj_w": w(ks[25], PLE_DIM, D_MODEL, 0.5),
    }


def reference(x, p, ffn1_norm, ffn1_w_gate, ffn1_w_up, ffn1_w_down, mix_norm, w_in,
              m_conv_w, m_conv_b, m_igate_b, m_fgate_b, m_out_norm, a_q_norm, a_k_norm,
              a_sinks, w_branch_m, w_branch_a, w_out, ffn2_norm, ffn2_w_gate, ffn2_w_up,
              ffn2_w_down, ple_norm, ple_gate_w, ple_proj_w):
    h = x
    for i in range(DEPTH):
        h = hybrid_layer(h, p[i], ffn1_norm[i], ffn1_w_gate[i], ffn1_w_up[i], ffn1_w_down[i],
                         mix_norm[i], w_in[i], m_conv_w[i], m_conv_b[i], m_igate_b[i],
                         m_fgate_b[i], m_out_norm[i], a_q_norm[i], a_k_norm[i], a_sinks[i],
                         w_branch_m[i], w_branch_a[i], w_out[i], ffn2_norm[i], ffn2_w_gate[i],
                         ffn2_w_up[i], ffn2_w_down[i], ple_norm[i], ple_gate_w[i], ple_proj_w[i])
    return h
```

```python
import numpy as np
import concourse.bass as bass
import concourse.mybir as mybir
from concourse.bass_utils import run_bass_kernel_spmd

F32 = mybir.dt.float32
BF16 = mybir.dt.bfloat16
AF = mybir.ActivationFunctionType
ALU = mybir.AluOpType
AX = mybir.AxisListType


COST_TABLE = {"act:148": 535, "act:179": 583, "act:180": 603, "act:181": 603, "act:185": 592, "act:186": 601, "act:191": 611, "act:223": 510, "act:251": 517, "act:254": 602, "act:257": 590, "act:258": 554, "act:273": 176, "act:276": 312, "act:277": 316, "act:286": 955, "act:309": 591, "act:334": 574, "act:336": 187, "act:338": 261, "act:339": 256, "act:343": 1004, "act:353": 512, "act:368": 583, "act:371": 575, "act:391": 521, "act:408": 553, "act:409": 547, "dve:128": 165, "dve:129": 141, "dve:132": 227, "dve:134": 112, "dve:135": 112, "dve:136": 537, "dve:147": 182, "dve:150": 487, "dve:151": 14, "dve:152": 29, "dve:153": 2192, "dve:154": 2192, "dve:195": 674, "dve:204": 692, "dve:205": 692, "dve:216": 687, "dve:222": 89, "dve:225": 486, "dve:228": 697, "dve:230": 78, "dve:233": 639, "dve:261": 360, "dve:263": 639, "dve:266": 222, "dve:269": 113, "dve:270": 106, "dve:271": 127, "dve:272": 101, "dve:278": 128, "dve:292": 292, "dve:293": 101, "dve:303": 225, "dve:313": 244, "dve:330": 372, "dve:332": 427, "dve:337": 220, "dve:340": 382, "dve:348": 683, "dve:356": 744, "dve:395": 684, "dve:406": 692, "dve:414": 392, "dve:419": 171, "dve:420": 193, "dve:433": 613, "dve:445": 680, "dve:450": 692, "dve:451": 693, "dve:454": 681, "dve:467": 669, "dve:468": 670, "pe:170": 247, "pe:192": 363, "pe:214": 236, "pe:285": 70, "pe:290": 87, "pe:312": 116, "pe:318": 99, "pe:319": 95, "pe:322": 79, "pe:323": 85, "pe:326": 219, "pe:345": 420, "pe:354": 522, "pe:370": 135, "pe:385": 116, "pe:400": 330, "pe:402": 350, "pe:466": 275}


class Buf:
    __slots__ = ("name", "w", "r", "excl")

    def __init__(self, name):
        self.name = name
        self.excl = name.startswith("p")
        self.w = None
        self.r = []


class _Op:
    __slots__ = ("eng", "fn", "deps", "dma", "needs_inc", "semval", "idx", "gidx", "odeps", "cost", "nbytes")


class Sched:
    ENGS = ("pe", "act", "dve", "pool", "sp")

    def __init__(self, nc):
        self.nc = nc
        self.streams = {e: [] for e in self.ENGS}
        self.dma_counts = {}
        self.nbuf = 0
        self.all_ops = []
        self.default_n = 512

    def buf(self, name=None):
        self.nbuf += 1
        return Buf(name or f"b{self.nbuf}")

    def op(self, eng, fn, reads=(), writes=(), dma=None, n=None, nbytes=262144):
        if n is None:
            n = self.default_n
        o = _Op()
        o.gidx = len(self.all_ops)
        self.all_ops.append(o)
        o.nbytes = nbytes
        if dma is not None:
            o.cost = 650.0 if eng == "pool" else 250.0
        elif eng == "pe":
            o.cost = 30.0 + 0.445 * n
        elif eng == "act":
            o.cost = 220.0 + 0.78 * n
        else:
            o.cost = 110.0 + 1.15 * n
        if dma is None:
            o.cost = float(COST_TABLE.get(f"{eng}:{fn.__code__.co_firstlineno - _BP_LINE}", o.cost))
        o.eng = eng
        o.fn = fn
        o.dma = dma
        o.needs_inc = False
        o.semval = None
        st = self.streams[eng]
        o.idx = len(st)
        deps = []
        for b in list(reads) + list(writes):
            if b.w is not None:
                deps.append(b.w)
        for b in writes:
            deps.extend(b.r)
        for b in reads:
            if b.excl:
                deps.extend(r for r in b.r if r[1] != eng)
        if dma is not None:
            c = self.dma_counts.get(dma, 0) + 1
            self.dma_counts[dma] = c
            ev = ("d", dma, c * 16, o)
        else:
            ev = ("e", eng, o.idx, o)
        o.odeps = list({id(d[3]): d[3] for d in deps}.values())
        seen = set()
        fd = []
        for d in deps:
            if d[0] == "e" and d[1] == eng and dma is None and eng == "pe":
                continue
            key = (d[0], d[1], d[2])
            if key in seen:
                continue
            seen.add(key)
            fd.append(d)
            if d[0] == "e":
                d[3].needs_inc = True
        o.deps = fd
        st.append(o)
        for b in writes:
            b.w = ev
            b.r = []
        for b in reads:
            if b not in writes:
                b.r.append(ev)
        return ev

    def schedule(self):
        import heapq
        ops = self.all_ops
        nops = len(ops)
        ndep = [0] * nops
        users = [[] for _ in range(nops)]
        for o in ops:
            ndep[o.gidx] = len(o.odeps)
            for d in o.odeps:
                users[d.gidx].append(o.gidx)
        ready_t = [0.0] * nops
        fin = [0.0] * nops
        ready = {e: [] for e in self.ENGS}
        for o in ops:
            if ndep[o.gidx] == 0:
                ready[o.eng].append(o.gidx)
        efree = {e: 0.0 for e in self.ENGS}
        pipe = 0.0
        order = {e: [] for e in self.ENGS}
        done = 0
        while done < nops:
            best = None
            for e in self.ENGS:
                rl = ready[e]
                if not rl:
                    continue
                t = efree[e]
                cand = None
                for g in rl:
                    st = ready_t[g] if ready_t[g] > t else t
                    key = (st, g)
                    if cand is None or key < cand[0]:
                        cand = (key, g)
                if best is None or cand[0] < best[0]:
                    best = (cand[0], cand[1], e)
            (st, _), g, e = best
            o = ops[g]
            ready[e].remove(g)
            if o.dma is not None:
                efree[e] = st + o.cost
                pipe = max(pipe, st + o.cost) + o.nbytes / 180.0
                f = pipe + 1800.0
            else:
                f = st + o.cost
                efree[e] = f
            fin[g] = f
            order[e].append(o)
            done += 1
            for ug in users[g]:
                ndep[ug] -= 1
                fh = f + (120.0 if ops[ug].eng != e else 60.0)
                if fh > ready_t[ug]:
                    ready_t[ug] = fh
                if ndep[ug] == 0:
                    ready[ops[ug].eng].append(ug)
        self.streams = order
        self.est_ns = max(fin) if fin else 0.0

    def emit(self, sems, dma_sems, final_waits=(), reorder=True):
        nc = self.nc
        if reorder:
            self.schedule()
        for e in self.ENGS:
            c = 0
            for o in self.streams[e]:
                if o.dma is None and o.needs_inc:
                    c += 1
                    o.semval = c
        engobj = {"pe": "tensor", "act": "scalar", "dve": "vector",
                  "pool": "gpsimd", "sp": "sync"}

        def run_stream(e, eng):
            waited = {}
            for o in self.streams[e]:
                for d in o.deps:
                    if d[0] == "e":
                        sem = sems[d[1]]
                        val = d[3].semval
                        k = ("e", d[1])
                    else:
                        sem = dma_sems[d[1]]
                        val = d[2]
                        k = ("d", d[1])
                    if waited.get(k, 0) >= val:
                        continue
                    waited[k] = val
                    eng.wait_ge(sem, val)
                ins = o.fn(eng)
                if o.dma is not None:
                    ins.then_inc(dma_sems[o.dma], 16)
                elif o.needs_inc:
                    ins.then_inc(sems[e], 1)
            if e == "sp":
                for key in final_waits:
                    eng.wait_ge(dma_sems[key], self.dma_counts[key] * 16)

        with nc.Block() as block:
            @block.tensor
            def _(eng):
                run_stream("pe", eng)

            @block.scalar
            def _(eng):
                run_stream("act", eng)

            @block.vector
            def _(eng):
                run_stream("dve", eng)

            @block.gpsimd
            def _(eng):
                run_stream("pool", eng)

            @block.sync
            def _(eng):
                run_stream("sp", eng)


D = 1024
DFF = 2752
NF = 22
TT = 512
NBLK = TT // 128
EPS = 1e-6
PLE = 256
N_IN = 7688
B_G1, B_U1, B_IN, B_A, B_B, B_O, B_G2, B_U2, B_PG = 0, 22, 44, 116, 124, 132, 140, 162, 184
NCHK = 192
V_N1, V_NM, V_N2, V_NP, V_MO, V_CW, V_CB, V_AQ, V_AK, V_SK, V_IB, V_FB = 0, 8, 16, 24, 32, 40, 104, 120, 121, 122, 138, 142
NV = 146


def _chunkify(w):
    k, n = w.shape
    return np.ascontiguousarray(w.reshape(k // 128, 128, n // 128, 128).transpose(2, 1, 0, 3)).reshape(n // 128, 128, (k // 128) * 128)


def _prep_layer(l, ins):
    f32 = np.float32
    def padc(w):
        o = np.zeros((w.shape[0], NF * 128), f32)
        o[:, :DFF] = w
        return o
    def padr(w):
        o = np.zeros((NF * 128, w.shape[1]), f32)
        o[:DFF] = w
        return o
    win = np.asarray(ins["w_in"][l], f32)
    cols = []
    cols.append(win[:, 0:4096])
    for j in range(8):
        cols.append(np.repeat(win[:, 4096 + j:4097 + j], 128, axis=1))
    cols.append(win[:, 4104:5128])
    for g in range(4):
        kk = win[:, 5128 + g * 64:5128 + (g + 1) * 64]
        cols += [kk, kk]
    for g in range(4):
        vv = win[:, 5384 + g * 64:5384 + (g + 1) * 64]
        cols += [vv, vv]
    cols.append(win[:, 5640:7688])
    winv = np.concatenate(cols, axis=1)
    assert winv.shape[1] == 72 * 128
    wk = np.concatenate([
        _chunkify(padc(np.asarray(ins["ffn1_w_gate"][l], f32))),
        _chunkify(padc(np.asarray(ins["ffn1_w_up"][l], f32))),
        _chunkify(winv),
        _chunkify(np.asarray(ins["w_branch_m"][l], f32)),
        _chunkify(np.asarray(ins["w_branch_a"][l], f32)),
        _chunkify(np.asarray(ins["w_out"][l], f32)),
        _chunkify(padc(np.asarray(ins["ffn2_w_gate"][l], f32))),
        _chunkify(padc(np.asarray(ins["ffn2_w_up"][l], f32))),
        _chunkify(np.asarray(ins["ple_gate_w"][l], f32)),
    ], axis=0)
    assert wk.shape == (NCHK, 128, 1024)
    wd = np.concatenate([_chunkify(padr(np.asarray(ins["ffn1_w_down"][l], f32))),
                         _chunkify(padr(np.asarray(ins["ffn2_w_down"][l], f32)))], axis=0)
    wp = _chunkify(np.asarray(ins["ple_proj_w"][l], f32))
    vec = np.zeros((128, NV), f32)
    def pc(v):
        return np.asarray(v, f32).reshape(-1, 128).T
    vec[:, V_N1:V_N1 + 8] = pc(ins["ffn1_norm"][l])
    vec[:, V_NM:V_NM + 8] = pc(ins["mix_norm"][l])
    vec[:, V_N2:V_N2 + 8] = pc(ins["ffn2_norm"][l])
    vec[:, V_NP:V_NP + 8] = pc(ins["ple_norm"][l])
    vec[:, V_MO:V_MO + 8] = pc(ins["m_out_norm"][l])
    cw = np.asarray(ins["m_conv_w"][l], f32)
    vec[:, V_CW:V_CW + 64] = cw.reshape(4, 16, 128).transpose(2, 1, 0).reshape(128, 64)
    vec[:, V_CB:V_CB + 16] = pc(ins["m_conv_b"][l])
    vec[:, V_AQ] = np.tile(np.asarray(ins["a_q_norm"][l], f32), 2)
    vec[:, V_AK] = np.tile(np.asarray(ins["a_k_norm"][l], f32), 2)
    vec[:, V_SK:V_SK + 16] = np.asarray(ins["a_sinks"][l], f32)[None, :]
    vec[:, V_IB:V_IB + 4] = np.asarray(ins["m_igate_b"][l], f32)[None, :]
    vec[:, V_FB:V_FB + 4] = np.asarray(ins["m_fgate_b"][l], f32)[None, :]
    return wk, wd, wp, vec


def build_program(T, L, debug_stop=None):
    import contextlib, os
    DBG = int(os.environ.get("KDBG", "0"))
    nc = bass.Bass("TRN2", target_bir_lowering=False)
    NT = T // TT
    xT = nc.dram_tensor("xT", [D, T], F32, kind="ExternalInput").ap()
    pTd = nc.dram_tensor("pTin", [L, PLE, T], F32, kind="ExternalInput").ap()
    WK = nc.dram_tensor("wk", [L, NCHK, 128, 1024], F32, kind="ExternalInput").ap()
    WD = nc.dram_tensor("wd", [L, 16, 128, NF * 128], F32, kind="ExternalInput").ap()
    WP = nc.dram_tensor("wp", [L, 8, 128, 256], F32, kind="ExternalInput").ap()
    VEC = nc.dram_tensor("vec", [L, 128, NV], F32, kind="ExternalInput").ap()
    outT = nc.dram_tensor("outT", [D, T], F32, kind="ExternalOutput").ap()
    WKs = nc.dram_tensor("wk_bf", [L, NCHK, 128, 1024], BF16, kind="Internal").ap()
    WDs = nc.dram_tensor("wd_bf", [L, 16, 128, NF * 128], BF16, kind="Internal").ap()
    WPs = nc.dram_tensor("wp_bf", [L, 8, 128, 256], BF16, kind="Internal").ap()
    S = Sched(nc)
    es = contextlib.ExitStack()
    with es:
        def sb(name, shape, dt=F32):
            return es.enter_context(nc.sbuf_tensor(name, shape, dt))

        def psum(name, shape, dt=F32):
            return es.enter_context(nc.psum_tensor(name, shape, dt))
        dma_keys = []

        def dkey(k):
            dma_keys.append(k)
            return k

        h = sb("h", [128, 8, TT]); bh = [S.buf(f"h{c}") for c in range(8)]
        u = sb("u", [128, 8, TT], BF16); bu = [S.buf(f"u{c}") for c in range(8)]
        hid = sb("hid", [128, NF, TT], BF16); bhid = [S.buf(f"hid{c}") for c in range(NF)]
        hm = hid[:, 0:8, :]; bhm = bhid[0:8]
        ha = hid[:, 8:16, :]; bha = bhid[8:16]
        mg = sb("mg", [128, 8, TT], BF16); bmg = [S.buf(f"mg{c}") for c in range(8)]
        ptb = sb("ptb", [128, 2, TT], BF16); bptb = S.buf("ptb"); dkey("ptb")
        vec = [sb(f"vec{l}", [128, NV]) for l in range(L)]; bvec = [S.buf(f"vec{l}") for l in range(L)]
        for l in range(L):
            dkey(f"vec{l}")
        nfb = [sb(f"nfb{l}", [128, 4]) for l in range(L)]
        skx = [sb(f"skx{l}", [128, 16]) for l in range(L)]
        ones32 = sb("ones32", [128, 128]); blk32 = sb("blk32", [128, 128]); id32 = sb("id32", [128, 128])
        onesb = sb("onesb", [128, 128], BF16); idb = sb("idb", [128, 128], BF16); blkb = sb("blkb", [128, 128], BF16)
        maskT = sb("maskT", [128, 128], BF16); swm = sb("swm", [128, 2, 2, 2, 128], BF16)
        bconst = S.buf("const")
        NW = 6
        wsl = [sb(f"wsl{i}", [128, 8, 128], BF16) for i in range(NW)]; bwsl = [S.buf(f"wsl{i}") for i in range(NW)]
        for i in range(NW):
            dkey(f"wsl{i}"); dkey(f"wst{i}"); dkey(f"wsl{i}h")
        bscr = {}
        state = {"tile": 0}

        def wload(kind, si, l, idx, slot_ap, slot_buf, src32, srcbf, semk, stk, nb, **kw):
            key = (kind, l, idx)
            if state["tile"] == 0:
                S.op("pool", lambda e: e.dma_start(out=slot_ap, in_=src32, **kw), writes=[slot_buf], dma=semk, nbytes=nb * 2)
                if NT > 1:
                    bscr[key] = S.buf(f"scr_{kind}_{l}_{idx}")
                    S.op("sp", lambda e: e.dma_start(out=srcbf, in_=slot_ap), reads=[slot_buf], writes=[bscr[key]], dma=stk, nbytes=nb)
            else:
                S.op("sp", lambda e: e.dma_start(out=slot_ap, in_=srcbf), reads=[bscr[key]], writes=[slot_buf], dma=semk + "h", nbytes=nb)
        wdsl = [sb(f"wdsl{i}", [128, NF, 128], BF16) for i in range(2)]; bwdsl = [S.buf(f"wdsl{i}") for i in range(2)]
        for i in range(2):
            dkey(f"wdsl{i}"); dkey(f"wdst{i}"); dkey(f"wdsl{i}h")
        wpsl = [sb(f"wpsl{i}", [128, 2, 128], BF16) for i in range(2)]; bwpsl = [S.buf(f"wpsl{i}") for i in range(2)]
        for i in range(2):
            dkey(f"wpsl{i}"); dkey(f"wpst{i}"); dkey(f"wpsl{i}h")
        dkey("hld"); dkey("hst")
        pg_ = [psum(f"pg{i}", [128, 512]) for i in range(3)]; bpg = [S.buf(f"pg{i}") for i in range(3)]
        pSb = [psum(f"pS{i}", [128, 512]) for i in range(2)]; bpSb = [S.buf(f"pS{i}") for i in range(2)]
        pA = psum("pA", [128, 512]); bpA = S.buf("pA")
        pB = psum("pB", [128, 512]); bpB = S.buf("pB")
        pT = psum("pT", [128, 1024], BF16); bpT = S.buf("pT")
        sq = [sb(f"sq{i}", [128, TT], BF16) for i in range(2)]; bsq = [S.buf(f"sq{i}") for i in range(2)]
        rstd = sb("rstd", [128, TT]); brstd = S.buf("rstd")
        t1 = [sb(f"t1_{i}", [128, TT]) for i in range(3)]; bt1 = [S.buf(f"t1_{i}") for i in range(3)]
        t2 = [sb(f"t2_{i}", [128, TT]) for i in range(4)]; bt2 = [S.buf(f"t2_{i}") for i in range(4)]
        ctmp = [sb(f"ctmp{i}", [128, TT + 3]) for i in range(2)]; bctmp = [S.buf(f"ctmp{i}") for i in range(2)]
        cacc = [sb(f"cacc{i}", [128, TT]) for i in range(2)]; bcacc = [S.buf(f"cacc{i}") for i in range(2)]
        ccar = [sb(f"ccar{l}", [128, 16, 3]) for l in range(L)]; bccar = [[S.buf(f"ccar{l}_{c}") for c in range(16)] for l in range(L)]
        CT = [[sb(f"CT{l}_{hd}", [128, 2, 257]) for hd in range(4)] for l in range(L)]
        bCT = [[S.buf(f"CT{l}_{hd}") for hd in range(4)] for l in range(L)]
        mst = [[sb(f"mst{l}_{hd}", [128, 1]) for hd in range(4)] for l in range(L)]
        bmst = [[S.buf(f"mst{l}_{hd}") for hd in range(4)] for l in range(L)]
        Cbf = sb("Cbf", [128, 2, 258], BF16); bCbf = S.buf("Cbf")
        qh_ = [sb(f"qh{i}", [128, 2, TT], BF16) for i in range(2)]; bqh_ = [S.buf(f"qh{i}") for i in range(2)]
        kh_ = [sb(f"kh{i}", [128, 2, TT], BF16) for i in range(2)]; bkh_ = [S.buf(f"kh{i}") for i in range(2)]
        vf_ = [sb(f"vf{i}", [128, 2, TT], BF16) for i in range(2)]; bvf_ = [S.buf(f"vf{i}") for i in range(2)]
        ktok_ = [sb(f"ktok{i}", [128, NBLK, 256], BF16) for i in range(2)]; bktok_ = [S.buf(f"ktok{i}") for i in range(2)]
        vtok_ = [sb(f"vtok{i}", [128, NBLK, 258], BF16) for i in range(2)]; bvtok_ = [S.buf(f"vtok{i}") for i in range(2)]
        ig_ = [sb("ig0", [128, TT])] * 2; bigs = [S.buf("ig0")] * 2
        cb_ = [sb("cb0", [128, TT])] * 2; bcb_ = [S.buf("cb0")] * 2
        gg_ = [sb("gg0", [128, TT])] * 2; bgg_ = [S.buf("gg0")] * 2
        erow_ = [sb("erow0", [128, TT])] * 2; berow_ = [S.buf("erow0")] * 2
        bnd_ = [sb(f"bnd{i}", [128, TT]) for i in range(2)]; bbnd_ = [S.buf(f"bnd{i}") for i in range(2)]
        sm_ = [sb(f"sm{i}", [128, 16]) for i in range(2)]; bsm_ = [S.buf(f"sm{i}") for i in range(2)]
        ecol_ = [sb(f"ecol{i}", [128, 4]) for i in range(2)]; becol_ = [S.buf(f"ecol{i}") for i in range(2)]
        junk = sb("junk", [128, 128]); bjunk = S.buf("junk")
        SmT = sb("SmT", [128, 128], BF16); bSmT = S.buf("SmT")
        rd = sb("rd", [128, 128]); brd = S.buf("rd")
        hr = sb("hr", [128, 2, TT]); bhr = S.buf("hr")
        hsq = sb("hsq", [128, 2, TT], BF16); bhsq = S.buf("hsq")
        rs = sb("rs", [128, TT]); brs = S.buf("rs")
        aqh_ = [sb(f"aqh{i}", [128, 2, TT], BF16) for i in range(2)]; baqh_ = [S.buf(f"aqh{i}") for i in range(2)]
        akc = [sb(f"akc{l}", [128, 4, 128 + TT], BF16) for l in range(L)]; bakc = [[S.buf(f"akc{l}_{g}") for g in range(4)] for l in range(L)]
        avt = [sb(f"avt{l}", [128, 4, NBLK + 1, 128], BF16) for l in range(L)]; bavt = [[S.buf(f"avt{l}_{g}") for g in range(4)] for l in range(L)]
        avf_ = [sb(f"avf{i}", [128, TT], BF16) for i in range(2)]; bavf_ = [S.buf(f"avf{i}") for i in range(2)]
        pTs_ = [sb(f"pTs{i}", [128, 1024], BF16) for i in range(2)]; bpTs_ = [S.buf(f"pTs{i}") for i in range(2)]
        dtot_ = [sb(f"dtot{i}", [128, 512]) for i in range(2)]; bdtot_ = [S.buf(f"dtot{i}") for i in range(2)]

        cnt = {"w": 0, "wd": 0, "wp": 0, "pg": 0, "pg5": 0, "sq": 0, "t1": 0, "t2": 0, "ct": 0}

        def rr(key, n):
            v = cnt[key] % n
            cnt[key] += 1
            return v

        S.op("dve", lambda e: e.memset(ones32[:], 1.0), writes=[bconst])
        S.op("dve", lambda e: e.memset(onesb[:], 1.0), writes=[bconst])
        S.op("pool", lambda e: e.memset(id32[:], 0.0), writes=[bconst])
        S.op("pool", lambda e: e.affine_select(out=id32[:], in_=id32[:], pattern=[[-1, 128]], compare_op=ALU.not_equal, fill=1.0, base=0, channel_multiplier=1), reads=[bconst], writes=[bconst])
        S.op("dve", lambda e: e.tensor_copy(out=idb[:], in_=id32[:]), reads=[bconst], writes=[bconst])
        S.op("pool", lambda e: e.memset(blk32[:], 0.0), writes=[bconst])
        S.op("dve", lambda e: e.memset(blk32[0:64, 0:64], 1.0), reads=[bconst], writes=[bconst])
        S.op("dve", lambda e: e.memset(blk32[64:128, 64:128], 1.0), reads=[bconst], writes=[bconst])
        S.op("dve", lambda e: e.tensor_copy(out=blkb[:], in_=blk32[:]), reads=[bconst], writes=[bconst])
        S.op("pool", lambda e: e.affine_select(out=maskT[:], in_=onesb[:], pattern=[[1, 128]], compare_op=ALU.is_ge, fill=0.0, base=0, channel_multiplier=-1), reads=[bconst], writes=[bconst])
        for hf in range(2):
            for a in range(2):
                S.op("pool", lambda e, hf=hf, a=a: e.affine_select(out=swm[:, hf, 0, a, :], in_=onesb[:], pattern=[[-1, 128]], compare_op=ALU.is_ge, fill=0.0, base=-1, channel_multiplier=1), reads=[bconst], writes=[bconst])
                S.op("pool", lambda e, hf=hf, a=a: e.affine_select(out=swm[:, hf, 1, a, :], in_=onesb[:], pattern=[[1, 128]], compare_op=ALU.is_ge, fill=0.0, base=0, channel_multiplier=-1), reads=[bconst], writes=[bconst])
        for l in range(L):
            S.op("sp", lambda e, l=l: e.dma_start(out=vec[l][:], in_=VEC[l]), writes=[bvec[l]], dma=f"vec{l}")
            S.op("dve", lambda e, l=l: e.tensor_scalar(out=nfb[l][:], in0=vec[l][:, V_FB:V_FB + 4], scalar1=-1.0, scalar2=None, op0=ALU.mult), reads=[bvec[l]], writes=[bvec[l]])
            S.op("act", lambda e, l=l: e.activation(out=skx[l][:], in_=vec[l][:, V_SK:V_SK + 16], func=AF.Exp), reads=[bvec[l]], writes=[bvec[l]])
            for hd in range(4):
                S.op("dve", lambda e, l=l, hd=hd: e.memset(CT[l][hd][:], 0.0), writes=[bCT[l][hd]])
                S.op("dve", lambda e, l=l, hd=hd: e.memset(mst[l][hd][:], 0.0), writes=[bmst[l][hd]])
            S.op("dve", lambda e, l=l: e.memset(ccar[l][:], 0.0), writes=bccar[l])
            S.op("dve", lambda e, l=l: e.memset(akc[l][:], 0.0), writes=bakc[l])
            S.op("dve", lambda e, l=l: e.memset(avt[l][:], 0.0), writes=bavt[l])

        ring5 = None

        def gemm(l, widx, rhs_t, rhs_bufs, big=False):
            si = rr("w", NW)
            wload("k", si, l, widx, wsl[si][:].rearrange("p k n -> p (k n)"), bwsl[si], WK[l, widx], WKs[l, widx], f"wsl{si}", f"wst{si}", 262144)
            if big:
                pi = rr("pg5", 5)
                pt_, bpt_ = (pg_ + pSb)[pi], (bpg + bpSb)[pi]
            else:
                pi = rr("pg", 3)
                pt_, bpt_ = pg_[pi], bpg[pi]
            for k in range(8):
                S.op("pe", lambda e, k=k: e.matmul(pt_[:], lhsT=wsl[si][:, k, :], rhs=rhs_t[:, k, :], start=(k == 0), stop=(k == 7)),
                     reads=[bwsl[si]] + list(rhs_bufs), writes=[bpt_])
            return pt_, bpt_

        def sigmoid_to(dst_ap, dst_buf, src_ap, src_buf, bias=None):
            ti = rr("t1", 3)
            n = src_ap.shape[-1]
            tt = t1[ti][:, 0:n]
            S.op("act", lambda e: e.activation(out=tt, in_=src_ap, func=AF.Exp, scale=-1.0), reads=[src_buf], writes=[bt1[ti]])
            S.op("act", lambda e: e.activation(out=tt, in_=tt, func=AF.Ln, bias=1.0), reads=[bt1[ti]], writes=[bt1[ti]])
            S.op("act", lambda e: e.activation(out=dst_ap, in_=tt, func=AF.Exp, scale=-1.0), reads=[bt1[ti]], writes=[dst_buf])

        def rsqrt_to(dst_ap, dst_buf, src_ap, src_buf, scale):
            S.op("act", lambda e: e.activation(out=dst_ap, in_=src_ap, func=AF.Ln, scale=scale, bias=EPS), reads=[src_buf], writes=[dst_buf])
            S.op("act", lambda e: e.activation(out=dst_ap, in_=dst_ap, func=AF.Exp, scale=-0.5), reads=[dst_buf], writes=[dst_buf])

        def norm(l, vcol):
            for c in range(8):
                si = rr("sq", 2)
                S.op("act", lambda e, c=c, si=si: e.activation(out=sq[si][:], in_=h[:, c, :], func=AF.Square), reads=[bh[c]], writes=[bsq[si]])
                S.op("pe", lambda e, c=c, si=si: e.matmul(pA[:], lhsT=onesb[:], rhs=sq[si][:], start=(c == 0), stop=(c == 7)), reads=[bsq[si], bconst], writes=[bpA])
            rsqrt_to(rstd[:], brstd, pA[:], bpA, 1.0 / D)
            for c in range(8):
                S.op("dve", lambda e, c=c: e.scalar_tensor_tensor(out=u[:, c, :], in0=h[:, c, :], scalar=vec[l][:, vcol + c:vcol + c + 1], in1=rstd[:], op0=ALU.mult, op1=ALU.mult),
                     reads=[bh[c], brstd, bvec[l]], writes=[bu[c]])

        def ffn(l, bg, bu_, wdbase):
            for f in range(NF):
                pgt, bpgt = gemm(l, bg + f, u, bu, big=True)
                put, bput = gemm(l, bu_ + f, u, bu, big=True)
                ti = rr("t2", 4)
                sigmoid_to(t2[ti][:], bt2[ti], pgt[:], bpgt)
                S.op("dve", lambda e, ti=ti, pgt=pgt: e.tensor_tensor(out=t2[ti][:], in0=pgt[:], in1=t2[ti][:], op=ALU.mult), reads=[bpgt, bt2[ti]], writes=[bt2[ti]])
                S.op("dve", lambda e, ti=ti, put=put, f=f: e.tensor_tensor(out=hid[:, f, :], in0=put[:], in1=t2[ti][:], op=ALU.mult), reads=[bput, bt2[ti]], writes=[bhid[f]])
            for d in range(8):
                si = rr("wd", 2)
                if state["tile"] == 0:
                    wload("d", si, l, wdbase + d, wdsl[si][:].rearrange("p k n -> p (k n)"), bwdsl[si], WD[l, wdbase + d], WDs[l, wdbase + d], f"wdsl{si}", f"wdst{si}", 720896, max_dma_last_dim=4096)
                else:
                    wload("d", si, l, wdbase + d, wdsl[si][:].rearrange("p k n -> p (k n)"), bwdsl[si], WD[l, wdbase + d], WDs[l, wdbase + d], f"wdsl{si}", f"wdst{si}", 720896)
                pi = rr("pg", 3)
                for k in range(NF):
                    S.op("pe", lambda e, k=k, si=si, pi=pi: e.matmul(pg_[pi][:], lhsT=wdsl[si][:, k, :], rhs=hid[:, k, :], start=(k == 0), stop=(k == NF - 1)),
                         reads=[bwdsl[si], bhid[k]], writes=[bpg[pi]])
                S.op("dve", lambda e, d=d, pi=pi: e.scalar_tensor_tensor(out=h[:, d, :], in0=pg_[pi][:], scalar=0.5, in1=h[:, d, :], op0=ALU.mult, op1=ALU.add),
                     reads=[bpg[pi], bh[d]], writes=[bh[d]])

        def conv_silu(l, c, pz, bpz, dst_ap, dst_buf, oscale):
            ci = rr("ct", 2)
            tm, btm, ac, bac = ctmp[ci], bctmp[ci], cacc[ci], bcacc[ci]
            S.op("dve", lambda e: e.tensor_copy(out=tm[:, 0:3], in_=ccar[l][:, c, :]), reads=[bccar[l][c]], writes=[btm], n=3)
            S.op("act", lambda e: e.copy(out=tm[:, 3:TT + 3], in_=pz[:]), reads=[bpz], writes=[btm])
            w = lambda j: vec[l][:, V_CW + c * 4 + j:V_CW + c * 4 + j + 1]
            S.op("dve", lambda e: e.tensor_scalar(out=ac[:], in0=tm[:, 3:TT + 3], scalar1=w(3), scalar2=vec[l][:, V_CB + c:V_CB + c + 1], op0=ALU.mult, op1=ALU.add),
                 reads=[btm, bvec[l]], writes=[bac])
            for j in (2, 1, 0):
                S.op("dve", lambda e, j=j: e.scalar_tensor_tensor(out=ac[:], in0=tm[:, j:j + TT], scalar=w(j), in1=ac[:], op0=ALU.mult, op1=ALU.add),
                     reads=[btm, bac, bvec[l]], writes=[bac])
            S.op("dve", lambda e: e.tensor_copy(out=ccar[l][:, c, :], in_=tm[:, TT:TT + 3]), reads=[btm], writes=[bccar[l][c]], n=3)
            ti = rr("t2", 4)
            sigmoid_to(t2[ti][:], bt2[ti], ac[:], bac)
            S.op("dve", lambda e: e.scalar_tensor_tensor(out=dst_ap, in0=ac[:], scalar=oscale, in1=t2[ti][:], op0=ALU.mult, op1=ALU.mult), reads=[bac, bt2[ti]], writes=[dst_buf])

        def mlstm_head(l, hd, ti_):
            pp_ = hd % 2
            qh, bqh, kh, bkh, vf, bvf = qh_[pp_], bqh_[pp_], kh_[pp_], bkh_[pp_], vf_[pp_], bvf_[pp_]
            ktok, bktok, vtok, bvtok = ktok_[pp_], bktok_[pp_], vtok_[pp_], bvtok_[pp_]
            ig, big_, cb, bcb, gg, bgg = ig_[pp_], bigs[pp_], cb_[pp_], bcb_[pp_], gg_[pp_], bgg_[pp_]
            erow, berow, bnd, bbnd, sm, bsm, ecol, becol = erow_[pp_], berow_[pp_], bnd_[pp_], bbnd_[pp_], sm_[pp_], bsm_[pp_], ecol_[pp_], becol_[pp_]
            for j in range(2):
                pz, bpz = gemm(l, B_IN + 2 * hd + j, u, bu)
                conv_silu(l, 2 * hd + j, pz, bpz, qh[:, j, :], bqh, 1.0 / 16.0)
            for j in range(2):
                pz, bpz = gemm(l, B_IN + 8 + 2 * hd + j, u, bu)
                conv_silu(l, 8 + 2 * hd + j, pz, bpz, kh[:, j, :], bkh, 1.0)
            for j in range(2):
                pz, bpz = gemm(l, B_IN + 16 + 2 * hd + j, u, bu)
                S.op("act", lambda e, j=j, pz=pz: e.copy(out=vf[:, j, :], in_=pz[:]), reads=[bpz], writes=[bvf])
            pz, bpz = gemm(l, B_IN + 32 + hd, u, bu)
            S.op("act", lambda e, pz=pz: e.activation(out=ig[:], in_=pz[:], func=AF.Identity, bias=vec[l][:, V_IB + hd:V_IB + hd + 1]), reads=[bpz, bvec[l]], writes=[big_])
            pz, bpz = gemm(l, B_IN + 36 + hd, u, bu)
            S.op("act", lambda e, pz=pz: e.activation(out=gg[:], in_=pz[:], func=AF.Exp, scale=-1.0, bias=nfb[l][:, hd:hd + 1]), reads=[bpz, bvec[l]], writes=[bgg])
            S.op("act", lambda e: e.activation(out=gg[:], in_=gg[:], func=AF.Ln, bias=1.0), reads=[bgg], writes=[bgg])
            for kk in range(NBLK):
                sl = slice(kk * 128, (kk + 1) * 128)
                S.op("dve", lambda e, sl=sl: e.tensor_tensor_scan(out=cb[:, sl], data0=ones32[:], data1=gg[:, sl], initial=0.0, op0=ALU.mult, op1=ALU.add), reads=[bgg, bconst], writes=[bcb], n=256)
            S.op("dve", lambda e: e.tensor_tensor(out=gg[:], in0=ig[:], in1=cb[:], op=ALU.add), reads=[big_, bcb, bgg], writes=[bgg])
            for kk in range(NBLK):
                sl = slice(kk * 128, (kk + 1) * 128)
                S.op("dve", lambda e, sl=sl, kk=kk: e.reduce_max(out=sm[:, kk:kk + 1], in_=gg[:, sl], axis=AX.X), reads=[bgg], writes=[bsm], n=128)
            m_ = mst[l][hd]; bm_ = bmst[l][hd]
            for kk in range(NBLK):
                S.op("dve", lambda e, kk=kk: e.tensor_tensor(out=sm[:, 4 + kk:5 + kk], in0=m_[:], in1=sm[:, kk:kk + 1], op=ALU.max), reads=[bm_, bsm], writes=[bsm], n=1)
                S.op("dve", lambda e, kk=kk: e.tensor_tensor(out=sm[:, 12 + kk:13 + kk], in0=m_[:], in1=sm[:, 4 + kk:5 + kk], op=ALU.subtract), reads=[bm_, bsm], writes=[bsm], n=1)
                S.op("dve", lambda e, kk=kk: e.tensor_tensor(out=m_[:], in0=sm[:, 4 + kk:5 + kk], in1=cb[:, kk * 128 + 127:kk * 128 + 128], op=ALU.subtract), reads=[bsm, bcb], writes=[bm_], n=1)
            S.op("dve", lambda e: e.tensor_scalar(out=sm[:, 8:12], in0=sm[:, 4:8], scalar1=-1.0, scalar2=None, op0=ALU.mult), reads=[bsm], writes=[bsm], n=4)
            S.op("act", lambda e: e.activation(out=sm[:, 12:16], in_=sm[:, 12:16], func=AF.Exp), reads=[bsm], writes=[bsm], n=4)
            for kk in range(NBLK):
                sl = slice(kk * 128, (kk + 1) * 128)
                S.op("act", lambda e, sl=sl, kk=kk: e.activation(out=erow[:, sl], in_=gg[:, sl], func=AF.Exp, bias=sm[:, 8 + kk:9 + kk]), reads=[bgg, bsm], writes=[berow], n=128)
                S.op("act", lambda e, sl=sl, kk=kk: e.activation(out=bnd[:, sl], in_=cb[:, sl], func=AF.Exp, bias=sm[:, 8 + kk:9 + kk]), reads=[bcb, bsm], writes=[bbnd], n=128)
                S.op("dve", lambda e, sl=sl, kk=kk: e.scalar_tensor_tensor(out=junk[:], in0=erow[:, sl], scalar=1.0, in1=id32[:], op0=ALU.mult, op1=ALU.mult, accum_out=ecol[:, kk:kk + 1]),
                     reads=[berow, bconst], writes=[bjunk, becol], n=200)
            if DBG == 1:
                return
            for b in range(NBLK):
                for j in range(2):
                    S.op("pe", lambda e, b=b, j=j: e.transpose(out=pT[:, b * 256 + j * 128:b * 256 + (j + 1) * 128], in_=kh[:, j, b * 128:(b + 1) * 128], identity=idb[:]), reads=[bkh, bconst], writes=[bpT], n=128)
            S.op("act", lambda e: e.copy(out=ktok[:].rearrange("p b n -> p (b n)"), in_=pT[:]), reads=[bpT], writes=[bktok], n=700)
            for b in range(NBLK):
                for j in range(2):
                    S.op("pe", lambda e, b=b, j=j: e.transpose(out=pT[:, b * 256 + j * 128:b * 256 + (j + 1) * 128], in_=vf[:, j, b * 128:(b + 1) * 128], identity=idb[:]), reads=[bvf, bconst], writes=[bpT], n=128)
            for b in range(NBLK):
                S.op("dve", lambda e, b=b: e.tensor_scalar(out=vtok[:, b, 0:256], in0=pT[:, b * 256:(b + 1) * 256], scalar1=ecol[:, b:b + 1], scalar2=None, op0=ALU.mult), reads=[bpT, becol], writes=[bvtok], n=256)
                S.op("dve", lambda e, b=b: e.tensor_copy(out=vtok[:, b, 256:257], in_=ecol[:, b:b + 1]), reads=[becol], writes=[bvtok], n=1)
            if DBG == 2:
                return
            C_, bC_ = CT[l][hd], bCT[l][hd]
            S.default_n = 128
            qd, bqd = vf, bvf
            for b in range(NBLK):
                if ti_ == 0 and b == 0:
                    continue
                S.op("dve", lambda e, b=b: e.tensor_scalar(out=qd[:, :, b * 128:(b + 1) * 128], in0=qh[:, :, b * 128:(b + 1) * 128], scalar1=sm[:, 12 + b:13 + b], scalar2=None, op0=ALU.mult),
                     reads=[bqh, bsm, bvf], writes=[bqd], n=256)
            for b in range(NBLK):
                first = (ti_ == 0 and b == 0)
                sl = slice(b * 128, (b + 1) * 128)
                if b == 0 and not first:
                    S.op("act", lambda e: e.copy(out=Cbf[:, :, 0:257], in_=C_[:]), reads=[bC_], writes=[bCbf], n=514)
                for j in range(2):
                    S.op("pe", lambda e, j=j, sl=sl: e.matmul(pA[:, 0:128], lhsT=kh[:, j, sl], rhs=qh[:, j, sl], start=(j == 0), stop=(j == 1)), reads=[bkh, bqh], writes=[bpA])
                S.op("dve", lambda e: e.tensor_tensor(out=SmT[:], in0=pA[:, 0:128], in1=maskT[:], op=ALU.mult), reads=[bpA, bconst], writes=[bSmT])
                for vc in range(2):
                    osl = slice(vc * 128, (vc + 1) * 128)
                    if not first:
                        for j in range(2):
                            S.op("pe", lambda e, j=j, osl=osl, sl=sl: e.matmul(pB[:, osl], lhsT=Cbf[:, j, osl], rhs=qd[:, j, sl], start=(j == 0), stop=False), reads=[bCbf, bqd], writes=[bpB])
                    S.op("pe", lambda e, osl=osl, b=b, first=first: e.matmul(pB[:, osl], lhsT=vtok[:, b, osl], rhs=SmT[:], start=first, stop=True), reads=[bvtok, bSmT], writes=[bpB])
                if not first:
                    for j in range(2):
                        S.op("pe", lambda e, j=j, sl=sl: e.matmul(pB[:, 256:384], lhsT=Cbf[:, j, 256:257].to_broadcast([128, 128]), rhs=qd[:, j, sl], start=(j == 0), stop=False), reads=[bCbf, bqd], writes=[bpB])
                S.op("pe", lambda e, b=b, first=first: e.matmul(pB[:, 256:384], lhsT=vtok[:, b, 256:257].to_broadcast([128, 128]), rhs=SmT[:], start=first, stop=True), reads=[bvtok, bSmT], writes=[bpB])
                for j in range(2):
                    S.op("pe", lambda e, j=j, b=b: e.matmul(pSb[j][:, 0:257], lhsT=ktok[:, b, j * 128:(j + 1) * 128], rhs=vtok[:, b, 0:257], start=True, stop=True), reads=[bktok, bvtok], writes=[bpSb[j]], n=257)
                for j in range(2):
                    if first:
                        S.op("dve", lambda e, j=j: e.tensor_copy(out=C_[:, j, :], in_=pSb[j][:, 0:257]), reads=[bpSb[j]], writes=[bC_], n=257)
                    else:
                        S.op("dve", lambda e, j=j, b=b: e.scalar_tensor_tensor(out=C_[:, j, :], in0=C_[:, j, :], scalar=sm[:, 12 + b:13 + b], in1=pSb[j][:, 0:257], op0=ALU.mult, op1=ALU.add), reads=[bpSb[j], bC_, bsm], writes=[bC_], n=257)
                if b < NBLK - 1:
                    S.op("act", lambda e: e.copy(out=Cbf[:, :, 0:257], in_=C_[:]), reads=[bC_], writes=[bCbf], n=514)
                S.op("act", lambda e: e.activation(out=rd[:], in_=pB[:, 256:384], func=AF.Abs), reads=[bpB], writes=[brd])
                S.op("dve", lambda e, sl=sl: e.tensor_tensor(out=rd[:], in0=rd[:], in1=bnd[:, sl], op=ALU.max), reads=[brd, bbnd], writes=[brd])
                S.op("act", lambda e: e.activation(out=rd[:], in_=rd[:], func=AF.Ln), reads=[brd], writes=[brd], n=128)
                S.op("act", lambda e: e.activation(out=rd[:], in_=rd[:], func=AF.Exp, scale=-1.0), reads=[brd], writes=[brd], n=128)
                S.op("dve", lambda e, sl=sl: e.tensor_tensor(out=hr[:, :, sl], in0=pB[:, 0:256].rearrange("p (a n) -> p a n", a=2), in1=rd[:].unsqueeze(1).to_broadcast([128, 2, 128]), op=ALU.mult), reads=[bpB, brd], writes=[bhr], n=256)
            S.default_n = 512
            S.op("act", lambda e: e.activation(out=hsq[:], in_=hr[:], func=AF.Square), reads=[bhr], writes=[bhsq], n=1024)
            for vc in range(2):
                S.op("pe", lambda e, vc=vc: e.matmul(pA[:], lhsT=onesb[:], rhs=hsq[:, vc, :], start=(vc == 0), stop=(vc == 1)), reads=[bhsq, bconst], writes=[bpA])
            rsqrt_to(rs[:], brs, pA[:], bpA, 1.0 / 256.0)
            for vc in range(2):
                S.op("dve", lambda e, vc=vc: e.scalar_tensor_tensor(out=hm[:, 2 * hd + vc, :], in0=hr[:, vc, :], scalar=vec[l][:, V_MO + 2 * hd + vc:V_MO + 2 * hd + vc + 1], in1=rs[:], op0=ALU.mult, op1=ALU.mult),
                     reads=[bhr, brs, bvec[l]], writes=[bhm[2 * hd + vc]])

        def qknorm(l, pz, bpz, gcol, dst_ap, dst_buf):
            si = rr("sq", 2)
            S.op("act", lambda e: e.activation(out=sq[si][:], in_=pz[:], func=AF.Square), reads=[bpz], writes=[bsq[si]])
            S.op("pe", lambda e: e.matmul(pA[:], lhsT=blkb[:], rhs=sq[si][:], start=True, stop=True), reads=[bsq[si], bconst], writes=[bpA])
            rsqrt_to(rstd[:], brstd, pA[:], bpA, 1.0 / 64.0)
            S.op("dve", lambda e: e.scalar_tensor_tensor(out=dst_ap, in0=pz[:], scalar=vec[l][:, gcol:gcol + 1], in1=rstd[:], op0=ALU.mult, op1=ALU.mult), reads=[bpz, brstd, bvec[l]], writes=[dst_buf])

        def swa_group(l, g, ti_):
            pp_ = g % 2
            aqh, baqh, avf, bavf = aqh_[pp_], baqh_[pp_], avf_[pp_], bavf_[pp_]
            pTs, bpTs, dtot, bdtot = pTs_[pp_], bpTs_[pp_], dtot_[pp_], bdtot_[pp_]
            for c in range(2):
                pz, bpz = gemm(l, B_IN + 40 + 2 * g + c, u, bu)
                qknorm(l, pz, bpz, V_AQ, aqh[:, c, :], baqh)
            pz, bpz = gemm(l, B_IN + 48 + g, u, bu)
            qknorm(l, pz, bpz, V_AK, akc[l][:, g, 128:128 + TT], bakc[l][g])
            pz, bpz = gemm(l, B_IN + 52 + g, u, bu)
            S.op("act", lambda e, pz=pz: e.copy(out=avf[:], in_=pz[:]), reads=[bpz], writes=[bavf])
            for b in range(NBLK):
                S.op("pe", lambda e, b=b: e.transpose(out=pT[:, b * 128:(b + 1) * 128], in_=avf[:, b * 128:(b + 1) * 128], identity=idb[:]), reads=[bavf, bconst], writes=[bpT], n=128)
            S.op("act", lambda e: e.copy(out=avt[l][:, g, 1:NBLK + 1, :].rearrange("p b n -> p (b n)"), in_=pT[:, 0:512]), reads=[bpT], writes=[bavt[l][g]])
            if DBG == 51:
                return
            for b in range(NBLK):
                first = (ti_ == 0 and b == 0)
                kbs = (1,) if first else (0, 1)
                qsl = slice(b * 128, (b + 1) * 128)
                for kb in kbs:
                    ksl = slice((b + kb) * 128, (b + kb + 1) * 128)
                    for j in range(4):
                        hf, a = j % 2, j // 2
                        col = kb * 256 + a * 128
                        S.op("pe", lambda e, hf=hf, a=a, col=col, ksl=ksl, qsl=qsl: e.matmul(pSb[hf][:, col:col + 128], lhsT=akc[l][hf * 64:(hf + 1) * 64, g, ksl], rhs=aqh[hf * 64:(hf + 1) * 64, a, qsl], start=True, stop=True),
                             reads=[bakc[l][g], baqh], writes=[bpSb[hf]], n=128)
                lo = 256 if first else 0
                if DBG == 52:
                    return
                for hf in range(2):
                    S.op("act", lambda e, lo=lo, hf=hf: e.activation(out=pTs[:, hf * 512 + lo:(hf + 1) * 512], in_=pSb[hf][:, lo:512], func=AF.Exp, scale=0.125), reads=[bpSb[hf]], writes=[bpTs])
                if DBG == 53:
                    continue
                pv = pTs[:].rearrange("p (hf r) -> p hf r", hf=2)
                S.op("dve", lambda e, lo=lo, pv=pv: e.tensor_tensor(out=pv[:, :, lo:512], in0=pv[:, :, lo:512], in1=swm[:].rearrange("p hf kb a q -> p hf (kb a q)")[:, :, lo:512], op=ALU.mult), reads=[bpTs, bconst], writes=[bpTs])
                if DBG == 54:
                    continue
                pk = pTs[:].rearrange("p (hf kb r) -> p hf kb r", hf=2, kb=2)
                for i_, kb in enumerate(kbs):
                    S.op("pe", lambda e, kb=kb, i_=i_, b=b, nk=len(kbs), pk=pk: e.matmul(pA[:], lhsT=avt[l][:, g, b + kb, :], rhs=pk[:, :, kb, :], start=(i_ == 0), stop=(i_ == nk - 1)), reads=[bavt[l][g], bpTs], writes=[bpA])
                for i_, kb in enumerate(kbs):
                    S.op("pe", lambda e, kb=kb, i_=i_, nk=len(kbs), pk=pk: e.matmul(pB[:], lhsT=onesb[:], rhs=pk[:, :, kb, :], start=(i_ == 0), stop=(i_ == nk - 1)), reads=[bconst, bpTs], writes=[bpB])
                if DBG == 55:
                    continue
                S.op("dve", lambda e: e.tensor_tensor(out=dtot[:].rearrange("p (hf a q) -> p hf a q", hf=2, a=2), in0=pB[:].rearrange("p (hf a q) -> p hf a q", hf=2, a=2),
                                                      in1=skx[l][:, 4 * g:4 * g + 4].rearrange("p (a hf) -> p hf a", hf=2).unsqueeze(3).to_broadcast([128, 2, 2, 128]), op=ALU.add), reads=[bpB, bvec[l]], writes=[bdtot])
                S.op("act", lambda e: e.activation(out=dtot[:], in_=dtot[:], func=AF.Ln), reads=[bdtot], writes=[bdtot])
                S.op("act", lambda e: e.activation(out=dtot[:], in_=dtot[:], func=AF.Exp, scale=-1.0), reads=[bdtot], writes=[bdtot])
                if DBG == 56:
                    continue
                for hf in range(2):
                    rows = slice(hf * 64, (hf + 1) * 64)
                    S.op("dve", lambda e, hf=hf, rows=rows, qsl=qsl: e.tensor_tensor(out=ha[rows, 2 * g:2 * g + 2, qsl],
                                                                                    in0=pA[rows, hf * 256:(hf + 1) * 256].rearrange("p (a q) -> p a q", a=2),
                                                                                    in1=dtot[rows, hf * 256:(hf + 1) * 256].rearrange("p (a q) -> p a q", a=2), op=ALU.mult),
                         reads=[bpA, bdtot], writes=[bha[2 * g], bha[2 * g + 1]], n=256)
            S.op("dve", lambda e: e.tensor_copy(out=akc[l][:, g, 0:128], in_=akc[l][:, g, TT:TT + 128]), reads=[bakc[l][g]], writes=[bakc[l][g]], n=128)
            S.op("dve", lambda e: e.tensor_copy(out=avt[l][:, g, 0, :], in_=avt[l][:, g, NBLK, :]), reads=[bavt[l][g]], writes=[bavt[l][g]], n=128)

        def mixer(l, ti_):
            for hd in range(4):
                mlstm_head(l, hd, ti_)
                if DBG in (1, 2, 3):
                    return
            if DBG == 4:
                return
            for c in range(8):
                pz, bpz = gemm(l, B_IN + 24 + c, u, bu)
                ti = rr("t2", 4)
                sigmoid_to(t2[ti][:], bt2[ti], pz[:], bpz)
                S.op("dve", lambda e, c=c, ti=ti: e.tensor_tensor(out=hm[:, c, :], in0=hm[:, c, :], in1=t2[ti][:], op=ALU.mult), reads=[bhm[c], bt2[ti]], writes=[bhm[c]])
            for g in range(4):
                swa_group(l, g, ti_)
                if DBG >= 5:
                    return
            if DBG == 6:
                return
            for d in range(8):
                pa_, bpa_ = gemm(l, B_A + d, hm, bhm)
                pgm, bpgm = gemm(l, B_IN + 56 + d, u, bu)
                ta = rr("t2", 4)
                sigmoid_to(t2[ta][:], bt2[ta], pgm[:], bpgm)
                S.op("dve", lambda e, ta=ta, pa_=pa_: e.tensor_tensor(out=t2[ta][:], in0=pa_[:], in1=t2[ta][:], op=ALU.mult), reads=[bpa_, bt2[ta]], writes=[bt2[ta]])
                pb_, bpb_ = gemm(l, B_B + d, ha, bha)
                pga, bpga = gemm(l, B_IN + 64 + d, u, bu)
                tb = rr("t2", 4)
                sigmoid_to(t2[tb][:], bt2[tb], pga[:], bpga)
                S.op("dve", lambda e, tb=tb, pb_=pb_: e.tensor_tensor(out=t2[tb][:], in0=pb_[:], in1=t2[tb][:], op=ALU.mult), reads=[bpb_, bt2[tb]], writes=[bt2[tb]])
                S.op("dve", lambda e, ta=ta, tb=tb, d=d: e.tensor_tensor(out=mg[:, d, :], in0=t2[ta][:], in1=t2[tb][:], op=ALU.add), reads=[bt2[ta], bt2[tb]], writes=[bmg[d]])
            for d in range(8):
                po, bpo = gemm(l, B_O + d, mg, bmg)
                S.op("dve", lambda e, d=d, po=po: e.tensor_tensor(out=h[:, d, :], in0=po[:], in1=h[:, d, :], op=ALU.add), reads=[bpo, bh[d]], writes=[bh[d]])

        def ple(l, t0):
            S.op("pool", lambda e: e.dma_start(out=ptb[:], in_=pTd[l, :, t0:t0 + TT].rearrange("(c p) t -> p c t", p=128)), writes=[bptb], dma="ptb")
            for d in range(8):
                pgt, bpgt = gemm(l, B_PG + d, u, bu)
                ti = rr("t2", 4)
                sigmoid_to(t2[ti][:], bt2[ti], pgt[:], bpgt)
                si = rr("wp", 2)
                wload("p", si, l, d, wpsl[si][:].rearrange("p k n -> p (k n)"), bwpsl[si], WP[l, d], WPs[l, d], f"wpsl{si}", f"wpst{si}", 65536)
                pi = rr("pg", 3)
                for k in range(2):
                    S.op("pe", lambda e, k=k, si=si, pi=pi: e.matmul(pg_[pi][:], lhsT=wpsl[si][:, k, :], rhs=ptb[:, k, :], start=(k == 0), stop=(k == 1)), reads=[bwpsl[si], bptb], writes=[bpg[pi]])
                S.op("dve", lambda e, ti=ti, pi=pi: e.tensor_tensor(out=t2[ti][:], in0=pg_[pi][:], in1=t2[ti][:], op=ALU.mult), reads=[bpg[pi], bt2[ti]], writes=[bt2[ti]])
                S.op("dve", lambda e, ti=ti, d=d: e.tensor_tensor(out=h[:, d, :], in0=h[:, d, :], in1=t2[ti][:], op=ALU.add), reads=[bh[d], bt2[ti]], writes=[bh[d]])

        for ti_ in range(NT):
            t0 = ti_ * TT
            state["tile"] = ti_
            S.op("sp", lambda e, t0=t0: e.dma_start(out=h[:], in_=xT[:, t0:t0 + TT].rearrange("(c p) t -> p c t", p=128)), writes=bh, dma="hld")
            for l in range(L):
                norm(l, V_N1)
                ffn(l, B_G1, B_U1, 0)
                if debug_stop == "ffn1":
                    break
                norm(l, V_NM)
                mixer(l, ti_)
                if debug_stop == "mix":
                    break
                norm(l, V_N2)
                ffn(l, B_G2, B_U2, 8)
                norm(l, V_NP)
                ple(l, t0)
            S.op("sp", lambda e, t0=t0: e.dma_start(out=outT[:, t0:t0 + TT].rearrange("(c p) t -> p c t", p=128), in_=h[:]), reads=bh, dma="hst")

        sems = {e: es.enter_context(nc.semaphore("s_" + e)) for e in Sched.ENGS}
        dsem = {k: es.enter_context(nc.semaphore("d_" + k)) for k in dma_keys if k in S.dma_counts}
        S.emit(sems, dsem, final_waits=["hst"])
    return nc


def _prep_inputs(inputs, L):
    per = [_prep_layer(l, inputs) for l in range(L)]
    wk = np.stack([p[0] for p in per]); wd = np.stack([p[1] for p in per])
    wp = np.stack([p[2] for p in per]); vec = np.stack([p[3] for p in per])
    return wk, wd, wp, vec


def kernel(**inputs):
    x = np.asarray(inputs["x"], np.float32)
    p = np.asarray(inputs["p"], np.float32)
    B, T, _ = x.shape
    L = p.shape[0]
    wk, wd, wp, vec = _prep_inputs(inputs, L)
    nc = build_program(T, L)
    in_maps = []
    for b in range(B):
        in_maps.append({"xT": np.ascontiguousarray(x[b].T), "pTin": np.ascontiguousarray(p[:, b].transpose(0, 2, 1)),
                        "wk": wk, "wd": wd, "wp": wp, "vec": vec})
    res = run_bass_kernel_spmd(nc, in_maps, core_ids=list(range(B)))
    out = np.stack([np.ascontiguousarray(res.results[b]["outT"].T) for b in range(B)])
    return out.astype(np.float32)


_BP_LINE = build_program.__code__.co_firstlineno
```

```python
import numpy as np
import concourse.bass as bass
import concourse.mybir as mybir
from concourse.bass_utils import run_bass_kernel_spmd

F32 = mybir.dt.float32
BF16 = mybir.dt.bfloat16
AF = mybir.ActivationFunctionType
ALU = mybir.AluOpType
AX = mybir.AxisListType


COST_TABLE = {"act:148": 535, "act:179": 583, "act:180": 603, "act:181": 603, "act:185": 592, "act:186": 601, "act:191": 611, "act:223": 510, "act:251": 517, "act:254": 602, "act:257": 590, "act:258": 554, "act:273": 176, "act:276": 312, "act:277": 316, "act:286": 955, "act:309": 591, "act:334": 574, "act:336": 187, "act:338": 261, "act:339": 256, "act:343": 1004, "act:353": 512, "act:368": 583, "act:371": 575, "act:391": 521, "act:408": 553, "act:409": 547, "dve:128": 165, "dve:129": 141, "dve:132": 227, "dve:134": 112, "dve:135": 112, "dve:136": 537, "dve:147": 182, "dve:150": 487, "dve:151": 14, "dve:152": 29, "dve:153": 2192, "dve:154": 2192, "dve:195": 674, "dve:204": 692, "dve:205": 692, "dve:216": 687, "dve:222": 89, "dve:225": 486, "dve:228": 697, "dve:230": 78, "dve:233": 639, "dve:261": 360, "dve:263": 639, "dve:266": 222, "dve:269": 113, "dve:270": 106, "dve:271": 127, "dve:272": 101, "dve:278": 128, "dve:292": 292, "dve:293": 101, "dve:303": 225, "dve:313": 244, "dve:330": 372, "dve:332": 427, "dve:337": 220, "dve:340": 382, "dve:348": 683, "dve:356": 744, "dve:395": 684, "dve:406": 692, "dve:414": 392, "dve:419": 171, "dve:420": 193, "dve:433": 613, "dve:445": 680, "dve:450": 692, "dve:451": 693, "dve:454": 681, "dve:467": 669, "dve:468": 670, "pe:170": 247, "pe:192": 363, "pe:214": 236, "pe:285": 70, "pe:290": 87, "pe:312": 116, "pe:318": 99, "pe:319": 95, "pe:322": 79, "pe:323": 85, "pe:326": 219, "pe:345": 420, "pe:354": 522, "pe:370": 135, "pe:385": 116, "pe:400": 330, "pe:402": 350, "pe:466": 275}


class Buf:
    __slots__ = ("name", "w", "r", "excl")

    def __init__(self, name):
        self.name = name
        self.excl = name.startswith("p")
        self.w = None
        self.r = []


class _Op:
    __slots__ = ("eng", "fn", "deps", "dma", "needs_inc", "semval", "idx", "gidx", "odeps", "cost", "nbytes")


class Sched:
    ENGS = ("pe", "act", "dve", "pool", "sp")

    def __init__(self, nc):
        self.nc = nc
        self.streams = {e: [] for e in self.ENGS}
        self.dma_counts = {}
        self.nbuf = 0
        self.all_ops = []
        self.default_n = 512

    def buf(self, name=None):
        self.nbuf += 1
        return Buf(name or f"b{self.nbuf}")

    def op(self, eng, fn, reads=(), writes=(), dma=None, n=None, nbytes=262144):
        if n is None:
            n = self.default_n
        o = _Op()
        o.gidx = len(self.all_ops)
        self.all_ops.append(o)
        o.nbytes = nbytes
        if dma is not None:
            o.cost = 650.0 if eng == "pool" else 250.0
        elif eng == "pe":
            o.cost = 30.0 + 0.445 * n
        elif eng == "act":
            o.cost = 220.0 + 0.78 * n
        else:
            o.cost = 110.0 + 1.15 * n
        if dma is None:
            o.cost = float(COST_TABLE.get(f"{eng}:{fn.__code__.co_firstlineno - _BP_LINE}", o.cost))
        o.eng = eng
        o.fn = fn
        o.dma = dma
        o.needs_inc = False
        o.semval = None
        st = self.streams[eng]
        o.idx = len(st)
        deps = []
        for b in list(reads) + list(writes):
            if b.w is not None:
                deps.append(b.w)
        for b in writes:
            deps.extend(b.r)
        for b in reads:
            if b.excl:
                deps.extend(r for r in b.r if r[1] != eng)
        if dma is not None:
            c = self.dma_counts.get(dma, 0) + 1
            self.dma_counts[dma] = c
            ev = ("d", dma, c * 16, o)
        else:
            ev = ("e", eng, o.idx, o)
        o.odeps = list({id(d[3]): d[3] for d in deps}.values())
        seen = set()
        fd = []
        for d in deps:
            if d[0] == "e" and d[1] == eng and dma is None and eng == "pe":
                continue
            key = (d[0], d[1], d[2])
            if key in seen:
                continue
            seen.add(key)
            fd.append(d)
            if d[0] == "e":
                d[3].needs_inc = True
        o.deps = fd
        st.append(o)
        for b in writes:
            b.w = ev
            b.r = []
        for b in reads:
            if b not in writes:
                b.r.append(ev)
        return ev

    def schedule(self):
        import heapq
        ops = self.all_ops
        nops = len(ops)
        ndep = [0] * nops
        users = [[] for _ in range(nops)]
        for o in ops:
            ndep[o.gidx] = len(o.odeps)
            for d in o.odeps:
                users[d.gidx].append(o.gidx)
        ready_t = [0.0] * nops
        fin = [0.0] * nops
        ready = {e: [] for e in self.ENGS}
        for o in ops:
            if ndep[o.gidx] == 0:
                ready[o.eng].append(o.gidx)
        efree = {e: 0.0 for e in self.ENGS}
        pipe = 0.0
        order = {e: [] for e in self.ENGS}
        done = 0
        while done < nops:
            best = None
            for e in self.ENGS:
                rl = ready[e]
                if not rl:
                    continue
                t = efree[e]
                cand = None
                for g in rl:
                    st = ready_t[g] if ready_t[g] > t else t
                    key = (st, g)
                    if cand is None or key < cand[0]:
                        cand = (key, g)
                if best is None or cand[0] < best[0]:
                    best = (cand[0], cand[1], e)
            (st, _), g, e = best
            o = ops[g]
            ready[e].remove(g)
            if o.dma is not None:
                efree[e] = st + o.cost
                pipe = max(pipe, st + o.cost) + o.nbytes / 180.0
                f = pipe + 1800.0
            else:
                f = st + o.cost
                efree[e] = f
            fin[g] = f
            order[e].append(o)
            done += 1
            for ug in users[g]:
                ndep[ug] -= 1
                fh = f + (120.0 if ops[ug].eng != e else 60.0)
                if fh > ready_t[ug]:
                    ready_t[ug] = fh
                if ndep[ug] == 0:
                    ready[ops[ug].eng].append(ug)
        self.streams = order
        self.est_ns = max(fin) if fin else 0.0

    def emit(self, sems, dma_sems, final_waits=(), reorder=True):
        nc = self.nc
        if reorder:
            self.schedule()
        for e in self.ENGS:
            c = 0
            for o in self.streams[e]:
                if o.dma is None and o.needs_inc:
                    c += 1
                    o.semval = c
        engobj = {"pe": "tensor", "act": "scalar", "dve": "vector",
                  "pool": "gpsimd", "sp": "sync"}

        def run_stream(e, eng):
            waited = {}
            for o in self.streams[e]:
                for d in o.deps:
                    if d[0] == "e":
                        sem = sems[d[1]]
                        val = d[3].semval
                        k = ("e", d[1])
                    else:
                        sem = dma_sems[d[1]]
                        val = d[2]
                        k = ("d", d[1])
                    if waited.get(k, 0) >= val:
                        continue
                    waited[k] = val
                    eng.wait_ge(sem, val)
                ins = o.fn(eng)
                if o.dma is not None:
                    ins.then_inc(dma_sems[o.dma], 16)
                elif o.needs_inc:
                    ins.then_inc(sems[e], 1)
            if e == "sp":
                for key in final_waits:
                    eng.wait_ge(dma_sems[key], self.dma_counts[key] * 16)

        with nc.Block() as block:
            @block.tensor
            def _(eng):
                run_stream("pe", eng)

            @block.scalar
            def _(eng):
                run_stream("act", eng)

            @block.vector
            def _(eng):
                run_stream("dve", eng)

            @block.gpsimd
            def _(eng):
                run_stream("pool", eng)

            @block.sync
            def _(eng):
                run_stream("sp", eng)


D = 1024
DFF = 2752
NF = 22
TT = 512
NBLK = TT // 128
EPS = 1e-6
PLE = 256
N_IN = 7688
B_G1, B_U1, B_IN, B_A, B_B, B_O, B_G2, B_U2, B_PG = 0, 22, 44, 116, 124, 132, 140, 162, 184
NCHK = 192
V_N1, V_NM, V_N2, V_NP, V_MO, V_CW, V_CB, V_AQ, V_AK, V_SK, V_IB, V_FB = 0, 8, 16, 24, 32, 40, 104, 120, 121, 122, 138, 142
NV = 146


def _chunkify(w):
    k, n = w.shape
    return np.ascontiguousarray(w.reshape(k // 128, 128, n // 128, 128).transpose(2, 1, 0, 3)).reshape(n // 128, 128, (k // 128) * 128)


def _prep_layer(l, ins):
    f32 = np.float32
    def padc(w):
        o = np.zeros((w.shape[0], NF * 128), f32)
        o[:, :DFF] = w
        return o
    def padr(w):
        o = np.zeros((NF * 128, w.shape[1]), f32)
        o[:DFF] = w
        return o
    win = np.asarray(ins["w_in"][l], f32)
    cols = []
    cols.append(win[:, 0:4096])
    for j in range(8):
        cols.append(np.repeat(win[:, 4096 + j:4097 + j], 128, axis=1))
    cols.append(win[:, 4104:5128])
    for g in range(4):
        kk = win[:, 5128 + g * 64:5128 + (g + 1) * 64]
        cols += [kk, kk]
    for g in range(4):
        vv = win[:, 5384 + g * 64:5384 + (g + 1) * 64]
        cols += [vv, vv]
    cols.append(win[:, 5640:7688])
    winv = np.concatenate(cols, axis=1)
    assert winv.shape[1] == 72 * 128
    wk = np.concatenate([
        _chunkify(padc(np.asarray(ins["ffn1_w_gate"][l], f32))),
        _chunkify(padc(np.asarray(ins["ffn1_w_up"][l], f32))),
        _chunkify(winv),
        _chunkify(np.asarray(ins["w_branch_m"][l], f32)),
        _chunkify(np.asarray(ins["w_branch_a"][l], f32)),
        _chunkify(np.asarray(ins["w_out"][l], f32)),
        _chunkify(padc(np.asarray(ins["ffn2_w_gate"][l], f32))),
        _chunkify(padc(np.asarray(ins["ffn2_w_up"][l], f32))),
        _chunkify(np.asarray(ins["ple_gate_w"][l], f32)),
    ], axis=0)
    assert wk.shape == (NCHK, 128, 1024)
    wd = np.concatenate([_chunkify(padr(np.asarray(ins["ffn1_w_down"][l], f32))),
                         _chunkify(padr(np.asarray(ins["ffn2_w_down"][l], f32)))], axis=0)
    wp = _chunkify(np.asarray(ins["ple_proj_w"][l], f32))
    vec = np.zeros((128, NV), f32)
    def pc(v):
        return np.asarray(v, f32).reshape(-1, 128).T
    vec[:, V_N1:V_N1 + 8] = pc(ins["ffn1_norm"][l])
    vec[:, V_NM:V_NM + 8] = pc(ins["mix_norm"][l])
    vec[:, V_N2:V_N2 + 8] = pc(ins["ffn2_norm"][l])
    vec[:, V_NP:V_NP + 8] = pc(ins["ple_norm"][l])
    vec[:, V_MO:V_MO + 8] = pc(ins["m_out_norm"][l])
    cw = np.asarray(ins["m_conv_w"][l], f32)
    vec[:, V_CW:V_CW + 64] = cw.reshape(4, 16, 128).transpose(2, 1, 0).reshape(128, 64)
    vec[:, V_CB:V_CB + 16] = pc(ins["m_conv_b"][l])
    vec[:, V_AQ] = np.tile(np.asarray(ins["a_q_norm"][l], f32), 2)
    vec[:, V_AK] = np.tile(np.asarray(ins["a_k_norm"][l], f32), 2)
    vec[:, V_SK:V_SK + 16] = np.asarray(ins["a_sinks"][l], f32)[None, :]
    vec[:, V_IB:V_IB + 4] = np.asarray(ins["m_igate_b"][l], f32)[None, :]
    vec[:, V_FB:V_FB + 4] = np.asarray(ins["m_fgate_b"][l], f32)[None, :]
    return wk, wd, wp, vec


def build_program(T, L, debug_stop=None):
    import contextlib, os
    DBG = int(os.environ.get("KDBG", "0"))
    nc = bass.Bass("TRN2", target_bir_lowering=False)
    NT = T // TT
    xT = nc.dram_tensor("xT", [D, T], F32, kind="ExternalInput").ap()
    pTd = nc.dram_tensor("pTin", [L, PLE, T], F32, kind="ExternalInput").ap()
    WK = nc.dram_tensor("wk", [L, NCHK, 128, 1024], F32, kind="ExternalInput").ap()
    WD = nc.dram_tensor("wd", [L, 16, 128, NF * 128], F32, kind="ExternalInput").ap()
    WP = nc.dram_tensor("wp", [L, 8, 128, 256], F32, kind="ExternalInput").ap()
    VEC = nc.dram_tensor("vec", [L, 128, NV], F32, kind="ExternalInput").ap()
    outT = nc.dram_tensor("outT", [D, T], F32, kind="ExternalOutput").ap()
    WKs = nc.dram_tensor("wk_bf", [L, NCHK, 128, 1024], BF16, kind="Internal").ap()
    WDs = nc.dram_tensor("wd_bf", [L, 16, 128, NF * 128], BF16, kind="Internal").ap()
    WPs = nc.dram_tensor("wp_bf", [L, 8, 128, 256], BF16, kind="Internal").ap()
    S = Sched(nc)
    es = contextlib.ExitStack()
    with es:
        def sb(name, shape, dt=F32):
            return es.enter_context(nc.sbuf_tensor(name, shape, dt))

        def psum(name, shape, dt=F32):
            return es.enter_context(nc.psum_tensor(name, shape, dt))
        dma_keys = []

        def dkey(k):
            dma_keys.append(k)
            return k

        h = sb("h", [128, 8, TT]); bh = [S.buf(f"h{c}") for c in range(8)]
        u = sb("u", [128, 8, TT], BF16); bu = [S.buf(f"u{c}") for c in range(8)]
        hid = sb("hid", [128, NF, TT], BF16); bhid = [S.buf(f"hid{c}") for c in range(NF)]
        hm = hid[:, 0:8, :]; bhm = bhid[0:8]
        ha = hid[:, 8:16, :]; bha = bhid[8:16]
        mg = sb("mg", [128, 8, TT], BF16); bmg = [S.buf(f"mg{c}") for c in range(8)]
        ptb = sb("ptb", [128, 2, TT], BF16); bptb = S.buf("ptb"); dkey("ptb")
        vec = [sb(f"vec{l}", [128, NV]) for l in range(L)]; bvec = [S.buf(f"vec{l}") for l in range(L)]
        for l in range(L):
            dkey(f"vec{l}")
        nfb = [sb(f"nfb{l}", [128, 4]) for l in range(L)]
        skx = [sb(f"skx{l}", [128, 16]) for l in range(L)]
        ones32 = sb("ones32", [128, 128]); blk32 = sb("blk32", [128, 128]); id32 = sb("id32", [128, 128])
        onesb = sb("onesb", [128, 128], BF16); idb = sb("idb", [128, 128], BF16); blkb = sb("blkb", [128, 128], BF16)
        maskT = sb("maskT", [128, 128], BF16); swm = sb("swm", [128, 2, 2, 2, 128], BF16)
        bconst = S.buf("const")
        NW = 6
        wsl = [sb(f"wsl{i}", [128, 8, 128], BF16) for i in range(NW)]; bwsl = [S.buf(f"wsl{i}") for i in range(NW)]
        for i in range(NW):
            dkey(f"wsl{i}"); dkey(f"wst{i}"); dkey(f"wsl{i}h")
        bscr = {}
        state = {"tile": 0}

        def wload(kind, si, l, idx, slot_ap, slot_buf, src32, srcbf, semk, stk, nb, **kw):
            key = (kind, l, idx)
            if state["tile"] == 0:
                S.op("pool", lambda e: e.dma_start(out=slot_ap, in_=src32, **kw), writes=[slot_buf], dma=semk, nbytes=nb * 2)
                if NT > 1:
                    bscr[key] = S.buf(f"scr_{kind}_{l}_{idx}")
                    S.op("sp", lambda e: e.dma_start(out=srcbf, in_=slot_ap), reads=[slot_buf], writes=[bscr[key]], dma=stk, nbytes=nb)
            else:
                S.op("sp", lambda e: e.dma_start(out=slot_ap, in_=srcbf), reads=[bscr[key]], writes=[slot_buf], dma=semk + "h", nbytes=nb)
        wdsl = [sb(f"wdsl{i}", [128, NF, 128], BF16) for i in range(2)]; bwdsl = [S.buf(f"wdsl{i}") for i in range(2)]
        for i in range(2):
            dkey(f"wdsl{i}"); dkey(f"wdst{i}"); dkey(f"wdsl{i}h")
        wpsl = [sb(f"wpsl{i}", [128, 2, 128], BF16) for i in range(2)]; bwpsl = [S.buf(f"wpsl{i}") for i in range(2)]
        for i in range(2):
            dkey(f"wpsl{i}"); dkey(f"wpst{i}"); dkey(f"wpsl{i}h")
        dkey("hld"); dkey("hst")
        pg_ = [psum(f"pg{i}", [128, 512]) for i in range(3)]; bpg = [S.buf(f"pg{i}") for i in range(3)]
        pSb = [psum(f"pS{i}", [128, 512]) for i in range(2)]; bpSb = [S.buf(f"pS{i}") for i in range(2)]
        pA = psum("pA", [128, 512]); bpA = S.buf("pA")
        pB = psum("pB", [128, 512]); bpB = S.buf("pB")
        pT = psum("pT", [128, 1024], BF16); bpT = S.buf("pT")
        sq = [sb(f"sq{i}", [128, TT], BF16) for i in range(2)]; bsq = [S.buf(f"sq{i}") for i in range(2)]
        rstd = sb("rstd", [128, TT]); brstd = S.buf("rstd")
        t1 = [sb(f"t1_{i}", [128, TT]) for i in range(3)]; bt1 = [S.buf(f"t1_{i}") for i in range(3)]
        t2 = [sb(f"t2_{i}", [128, TT]) for i in range(4)]; bt2 = [S.buf(f"t2_{i}") for i in range(4)]
        ctmp = [sb(f"ctmp{i}", [128, TT + 3]) for i in range(2)]; bctmp = [S.buf(f"ctmp{i}") for i in range(2)]
        cacc = [sb(f"cacc{i}", [128, TT]) for i in range(2)]; bcacc = [S.buf(f"cacc{i}") for i in range(2)]
        ccar = [sb(f"ccar{l}", [128, 16, 3]) for l in range(L)]; bccar = [[S.buf(f"ccar{l}_{c}") for c in range(16)] for l in range(L)]
        CT = [[sb(f"CT{l}_{hd}", [128, 2, 257]) for hd in range(4)] for l in range(L)]
        bCT = [[S.buf(f"CT{l}_{hd}") for hd in range(4)] for l in range(L)]
        mst = [[sb(f"mst{l}_{hd}", [128, 1]) for hd in range(4)] for l in range(L)]
        bmst = [[S.buf(f"mst{l}_{hd}") for hd in range(4)] for l in range(L)]
        Cbf = sb("Cbf", [128, 2, 258], BF16); bCbf = S.buf("Cbf")
        qh_ = [sb(f"qh{i}", [128, 2, TT], BF16) for i in range(2)]; bqh_ = [S.buf(f"qh{i}") for i in range(2)]
        kh_ = [sb(f"kh{i}", [128, 2, TT], BF16) for i in range(2)]; bkh_ = [S.buf(f"kh{i}") for i in range(2)]
        vf_ = [sb(f"vf{i}", [128, 2, TT], BF16) for i in range(2)]; bvf_ = [S.buf(f"vf{i}") for i in range(2)]
        ktok_ = [sb(f"ktok{i}", [128, NBLK, 256], BF16) for i in range(2)]; bktok_ = [S.buf(f"ktok{i}") for i in range(2)]
        vtok_ = [sb(f"vtok{i}", [128, NBLK, 258], BF16) for i in range(2)]; bvtok_ = [S.buf(f"vtok{i}") for i in range(2)]
        ig_ = [sb("ig0", [128, TT])] * 2; bigs = [S.buf("ig0")] * 2
        cb_ = [sb("cb0", [128, TT])] * 2; bcb_ = [S.buf("cb0")] * 2
        gg_ = [sb("gg0", [128, TT])] * 2; bgg_ = [S.buf("gg0")] * 2
        erow_ = [sb("erow0", [128, TT])] * 2; berow_ = [S.buf("erow0")] * 2
        bnd_ = [sb(f"bnd{i}", [128, TT]) for i in range(2)]; bbnd_ = [S.buf(f"bnd{i}") for i in range(2)]
        sm_ = [sb(f"sm{i}", [128, 16]) for i in range(2)]; bsm_ = [S.buf(f"sm{i}") for i in range(2)]
        ecol_ = [sb(f"ecol{i}", [128, 4]) for i in range(2)]; becol_ = [S.buf(f"ecol{i}") for i in range(2)]
        junk = sb("junk", [128, 128]); bjunk = S.buf("junk")
        SmT = sb("SmT", [128, 128], BF16); bSmT = S.buf("SmT")
        rd = sb("rd", [128, 128]); brd = S.buf("rd")
        hr = sb("hr", [128, 2, TT]); bhr = S.buf("hr")
        hsq = sb("hsq", [128, 2, TT], BF16); bhsq = S.buf("hsq")
        rs = sb("rs", [128, TT]); brs = S.buf("rs")
        aqh_ = [sb(f"aqh{i}", [128, 2, TT], BF16) for i in range(2)]; baqh_ = [S.buf(f"aqh{i}") for i in range(2)]
        akc = [sb(f"akc{l}", [128, 4, 128 + TT], BF16) for l in range(L)]; bakc = [[S.buf(f"akc{l}_{g}") for g in range(4)] for l in range(L)]
        avt = [sb(f"avt{l}", [128, 4, NBLK + 1, 128], BF16) for l in range(L)]; bavt = [[S.buf(f"avt{l}_{g}") for g in range(4)] for l in range(L)]
        avf_ = [sb(f"avf{i}", [128, TT], BF16) for i in range(2)]; bavf_ = [S.buf(f"avf{i}") for i in range(2)]
        pTs_ = [sb(f"pTs{i}", [128, 1024], BF16) for i in range(2)]; bpTs_ = [S.buf(f"pTs{i}") for i in range(2)]
        dtot_ = [sb(f"dtot{i}", [128, 512]) for i in range(2)]; bdtot_ = [S.buf(f"dtot{i}") for i in range(2)]

        cnt = {"w": 0, "wd": 0, "wp": 0, "pg": 0, "pg5": 0, "sq": 0, "t1": 0, "t2": 0, "ct": 0}

        def rr(key, n):
            v = cnt[key] % n
            cnt[key] += 1
            return v

        S.op("dve", lambda e: e.memset(ones32[:], 1.0), writes=[bconst])
        S.op("dve", lambda e: e.memset(onesb[:], 1.0), writes=[bconst])
        S.op("pool", lambda e: e.memset(id32[:], 0.0), writes=[bconst])
        S.op("pool", lambda e: e.affine_select(out=id32[:], in_=id32[:], pattern=[[-1, 128]], compare_op=ALU.not_equal, fill=1.0, base=0, channel_multiplier=1), reads=[bconst], writes=[bconst])
        S.op("dve", lambda e: e.tensor_copy(out=idb[:], in_=id32[:]), reads=[bconst], writes=[bconst])
        S.op("pool", lambda e: e.memset(blk32[:], 0.0), writes=[bconst])
        S.op("dve", lambda e: e.memset(blk32[0:64, 0:64], 1.0), reads=[bconst], writes=[bconst])
        S.op("dve", lambda e: e.memset(blk32[64:128, 64:128], 1.0), reads=[bconst], writes=[bconst])
        S.op("dve", lambda e: e.tensor_copy(out=blkb[:], in_=blk32[:]), reads=[bconst], writes=[bconst])
        S.op("pool", lambda e: e.affine_select(out=maskT[:], in_=onesb[:], pattern=[[1, 128]], compare_op=ALU.is_ge, fill=0.0, base=0, channel_multiplier=-1), reads=[bconst], writes=[bconst])
        for hf in range(2):
            for a in range(2):
                S.op("pool", lambda e, hf=hf, a=a: e.affine_select(out=swm[:, hf, 0, a, :], in_=onesb[:], pattern=[[-1, 128]], compare_op=ALU.is_ge, fill=0.0, base=-1, channel_multiplier=1), reads=[bconst], writes=[bconst])
                S.op("pool", lambda e, hf=hf, a=a: e.affine_select(out=swm[:, hf, 1, a, :], in_=onesb[:], pattern=[[1, 128]], compare_op=ALU.is_ge, fill=0.0, base=0, channel_multiplier=-1), reads=[bconst], writes=[bconst])
        for l in range(L):
            S.op("sp", lambda e, l=l: e.dma_start(out=vec[l][:], in_=VEC[l]), writes=[bvec[l]], dma=f"vec{l}")
            S.op("dve", lambda e, l=l: e.tensor_scalar(out=nfb[l][:], in0=vec[l][:, V_FB:V_FB + 4], scalar1=-1.0, scalar2=None, op0=ALU.mult), reads=[bvec[l]], writes=[bvec[l]])
            S.op("act", lambda e, l=l: e.activation(out=skx[l][:], in_=vec[l][:, V_SK:V_SK + 16], func=AF.Exp), reads=[bvec[l]], writes=[bvec[l]])
            for hd in range(4):
                S.op("dve", lambda e, l=l, hd=hd: e.memset(CT[l][hd][:], 0.0), writes=[bCT[l][hd]])
                S.op("dve", lambda e, l=l, hd=hd: e.memset(mst[l][hd][:], 0.0), writes=[bmst[l][hd]])
            S.op("dve", lambda e, l=l: e.memset(ccar[l][:], 0.0), writes=bccar[l])
            S.op("dve", lambda e, l=l: e.memset(akc[l][:], 0.0), writes=bakc[l])
            S.op("dve", lambda e, l=l: e.memset(avt[l][:], 0.0), writes=bavt[l])

        ring5 = None

        def gemm(l, widx, rhs_t, rhs_bufs, big=False):
            si = rr("w", NW)
            wload("k", si, l, widx, wsl[si][:].rearrange("p k n -> p (k n)"), bwsl[si], WK[l, widx], WKs[l, widx], f"wsl{si}", f"wst{si}", 262144)
            if big:
                pi = rr("pg5", 5)
                pt_, bpt_ = (pg_ + pSb)[pi], (bpg + bpSb)[pi]
            else:
                pi = rr("pg", 3)
                pt_, bpt_ = pg_[pi], bpg[pi]
            for k in range(8):
                S.op("pe", lambda e, k=k: e.matmul(pt_[:], lhsT=wsl[si][:, k, :], rhs=rhs_t[:, k, :], start=(k == 0), stop=(k == 7)),
                     reads=[bwsl[si]] + list(rhs_bufs), writes=[bpt_])
            return pt_, bpt_

        def sigmoid_to(dst_ap, dst_buf, src_ap, src_buf, bias=None):
            ti = rr("t1", 3)
            n = src_ap.shape[-1]
            tt = t1[ti][:, 0:n]
            S.op("act", lambda e: e.activation(out=tt, in_=src_ap, func=AF.Exp, scale=-1.0), reads=[src_buf], writes=[bt1[ti]])
            S.op("act", lambda e: e.activation(out=tt, in_=tt, func=AF.Ln, bias=1.0), reads=[bt1[ti]], writes=[bt1[ti]])
            S.op("act", lambda e: e.activation(out=dst_ap, in_=tt, func=AF.Exp, scale=-1.0), reads=[bt1[ti]], writes=[dst_buf])

        def rsqrt_to(dst_ap, dst_buf, src_ap, src_buf, scale):
            S.op("act", lambda e: e.activation(out=dst_ap, in_=src_ap, func=AF.Ln, scale=scale, bias=EPS), reads=[src_buf], writes=[dst_buf])
            S.op("act", lambda e: e.activation(out=dst_ap, in_=dst_ap, func=AF.Exp, scale=-0.5), reads=[dst_buf], writes=[dst_buf])

        def norm(l, vcol):
            for c in range(8):
                si = rr("sq", 2)
                S.op("act", lambda e, c=c, si=si: e.activation(out=sq[si][:], in_=h[:, c, :], func=AF.Square), reads=[bh[c]], writes=[bsq[si]])
                S.op("pe", lambda e, c=c, si=si: e.matmul(pA[:], lhsT=onesb[:], rhs=sq[si][:], start=(c == 0), stop=(c == 7)), reads=[bsq[si], bconst], writes=[bpA])
            rsqrt_to(rstd[:], brstd, pA[:], bpA, 1.0 / D)
            for c in range(8):
                S.op("dve", lambda e, c=c: e.scalar_tensor_tensor(out=u[:, c, :], in0=h[:, c, :], scalar=vec[l][:, vcol + c:vcol + c + 1], in1=rstd[:], op0=ALU.mult, op1=ALU.mult),
                     reads=[bh[c], brstd, bvec[l]], writes=[bu[c]])

        def ffn(l, bg, bu_, wdbase):
            for f in range(NF):
                pgt, bpgt = gemm(l, bg + f, u, bu, big=True)
                put, bput = gemm(l, bu_ + f, u, bu, big=True)
                ti = rr("t2", 4)
                sigmoid_to(t2[ti][:], bt2[ti], pgt[:], bpgt)
                S.op("dve", lambda e, ti=ti, pgt=pgt: e.tensor_tensor(out=t2[ti][:], in0=pgt[:], in1=t2[ti][:], op=ALU.mult), reads=[bpgt, bt2[ti]], writes=[bt2[ti]])
                S.op("dve", lambda e, ti=ti, put=put, f=f: e.tensor_tensor(out=hid[:, f, :], in0=put[:], in1=t2[ti][:], op=ALU.mult), reads=[bput, bt2[ti]], writes=[bhid[f]])
            for d in range(8):
                si = rr("wd", 2)
                if state["tile"] == 0:
                    wload("d", si, l, wdbase + d, wdsl[si][:].rearrange("p k n -> p (k n)"), bwdsl[si], WD[l, wdbase + d], WDs[l, wdbase + d], f"wdsl{si}", f"wdst{si}", 720896, max_dma_last_dim=4096)
                else:
                    wload("d", si, l, wdbase + d, wdsl[si][:].rearrange("p k n -> p (k n)"), bwdsl[si], WD[l, wdbase + d], WDs[l, wdbase + d], f"wdsl{si}", f"wdst{si}", 720896)
                pi = rr("pg", 3)
                for k in range(NF):
                    S.op("pe", lambda e, k=k, si=si, pi=pi: e.matmul(pg_[pi][:], lhsT=wdsl[si][:, k, :], rhs=hid[:, k, :], start=(k == 0), stop=(k == NF - 1)),
                         reads=[bwdsl[si], bhid[k]], writes=[bpg[pi]])
                S.op("dve", lambda e, d=d, pi=pi: e.scalar_tensor_tensor(out=h[:, d, :], in0=pg_[pi][:], scalar=0.5, in1=h[:, d, :], op0=ALU.mult, op1=ALU.add),
                     reads=[bpg[pi], bh[d]], writes=[bh[d]])

        def conv_silu(l, c, pz, bpz, dst_ap, dst_buf, oscale):
            ci = rr("ct", 2)
            tm, btm, ac, bac = ctmp[ci], bctmp[ci], cacc[ci], bcacc[ci]
            S.op("dve", lambda e: e.tensor_copy(out=tm[:, 0:3], in_=ccar[l][:, c, :]), reads=[bccar[l][c]], writes=[btm], n=3)
            S.op("act", lambda e: e.copy(out=tm[:, 3:TT + 3], in_=pz[:]), reads=[bpz], writes=[btm])
            w = lambda j: vec[l][:, V_CW + c * 4 + j:V_CW + c * 4 + j + 1]
            S.op("dve", lambda e: e.tensor_scalar(out=ac[:], in0=tm[:, 3:TT + 3], scalar1=w(3), scalar2=vec[l][:, V_CB + c:V_CB + c + 1], op0=ALU.mult, op1=ALU.add),
                 reads=[btm, bvec[l]], writes=[bac])
            for j in (2, 1, 0):
                S.op("dve", lambda e, j=j: e.scalar_tensor_tensor(out=ac[:], in0=tm[:, j:j + TT], scalar=w(j), in1=ac[:], op0=ALU.mult, op1=ALU.add),
                     reads=[btm, bac, bvec[l]], writes=[bac])
            S.op("dve", lambda e: e.tensor_copy(out=ccar[l][:, c, :], in_=tm[:, TT:TT + 3]), reads=[btm], writes=[bccar[l][c]], n=3)
            ti = rr("t2", 4)
            sigmoid_to(t2[ti][:], bt2[ti], ac[:], bac)
            S.op("dve", lambda e: e.scalar_tensor_tensor(out=dst_ap, in0=ac[:], scalar=oscale, in1=t2[ti][:], op0=ALU.mult, op1=ALU.mult), reads=[bac, bt2[ti]], writes=[dst_buf])

        def mlstm_head(l, hd, ti_):
            pp_ = hd % 2
            qh, bqh, kh, bkh, vf, bvf = qh_[pp_], bqh_[pp_], kh_[pp_], bkh_[pp_], vf_[pp_], bvf_[pp_]
            ktok, bktok, vtok, bvtok = ktok_[pp_], bktok_[pp_], vtok_[pp_], bvtok_[pp_]
            ig, big_, cb, bcb, gg, bgg = ig_[pp_], bigs[pp_], cb_[pp_], bcb_[pp_], gg_[pp_], bgg_[pp_]
            erow, berow, bnd, bbnd, sm, bsm, ecol, becol = erow_[pp_], berow_[pp_], bnd_[pp_], bbnd_[pp_], sm_[pp_], bsm_[pp_], ecol_[pp_], becol_[pp_]
            for j in range(2):
                pz, bpz = gemm(l, B_IN + 2 * hd + j, u, bu)
                conv_silu(l, 2 * hd + j, pz, bpz, qh[:, j, :], bqh, 1.0 / 16.0)
            for j in range(2):
                pz, bpz = gemm(l, B_IN + 8 + 2 * hd + j, u, bu)
                conv_silu(l, 8 + 2 * hd + j, pz, bpz, kh[:, j, :], bkh, 1.0)
            for j in range(2):
                pz, bpz = gemm(l, B_IN + 16 + 2 * hd + j, u, bu)
                S.op("act", lambda e, j=j, pz=pz: e.copy(out=vf[:, j, :], in_=pz[:]), reads=[bpz], writes=[bvf])
            pz, bpz = gemm(l, B_IN + 32 + hd, u, bu)
            S.op("act", lambda e, pz=pz: e.activation(out=ig[:], in_=pz[:], func=AF.Identity, bias=vec[l][:, V_IB + hd:V_IB + hd + 1]), reads=[bpz, bvec[l]], writes=[big_])
            pz, bpz = gemm(l, B_IN + 36 + hd, u, bu)
            S.op("act", lambda e, pz=pz: e.activation(out=gg[:], in_=pz[:], func=AF.Exp, scale=-1.0, bias=nfb[l][:, hd:hd + 1]), reads=[bpz, bvec[l]], writes=[bgg])
            S.op("act", lambda e: e.activation(out=gg[:], in_=gg[:], func=AF.Ln, bias=1.0), reads=[bgg], writes=[bgg])
            for kk in range(NBLK):
                sl = slice(kk * 128, (kk + 1) * 128)
                S.op("dve", lambda e, sl=sl: e.tensor_tensor_scan(out=cb[:, sl], data0=ones32[:], data1=gg[:, sl], initial=0.0, op0=ALU.mult, op1=ALU.add), reads=[bgg, bconst], writes=[bcb], n=256)
            S.op("dve", lambda e: e.tensor_tensor(out=gg[:], in0=ig[:], in1=cb[:], op=ALU.add), reads=[big_, bcb, bgg], writes=[bgg])
            for kk in range(NBLK):
                sl = slice(kk * 128, (kk + 1) * 128)
                S.op("dve", lambda e, sl=sl, kk=kk: e.reduce_max(out=sm[:, kk:kk + 1], in_=gg[:, sl], axis=AX.X), reads=[bgg], writes=[bsm], n=128)
            m_ = mst[l][hd]; bm_ = bmst[l][hd]
            for kk in range(NBLK):
                S.op("dve", lambda e, kk=kk: e.tensor_tensor(out=sm[:, 4 + kk:5 + kk], in0=m_[:], in1=sm[:, kk:kk + 1], op=ALU.max), reads=[bm_, bsm], writes=[bsm], n=1)
                S.op("dve", lambda e, kk=kk: e.tensor_tensor(out=sm[:, 12 + kk:13 + kk], in0=m_[:], in1=sm[:, 4 + kk:5 + kk], op=ALU.subtract), reads=[bm_, bsm], writes=[bsm], n=1)
                S.op("dve", lambda e, kk=kk: e.tensor_tensor(out=m_[:], in0=sm[:, 4 + kk:5 + kk], in1=cb[:, kk * 128 + 127:kk * 128 + 128], op=ALU.subtract), reads=[bsm, bcb], writes=[bm_], n=1)
            S.op("dve", lambda e: e.tensor_scalar(out=sm[:, 8:12], in0=sm[:, 4:8], scalar1=-1.0, scalar2=None, op0=ALU.mult), reads=[bsm], writes=[bsm], n=4)
            S.op("act", lambda e: e.activation(out=sm[:, 12:16], in_=sm[:, 12:16], func=AF.Exp), reads=[bsm], writes=[bsm], n=4)
            for kk in range(NBLK):
                sl = slice(kk * 128, (kk + 1) * 128)
                S.op("act", lambda e, sl=sl, kk=kk: e.activation(out=erow[:, sl], in_=gg[:, sl], func=AF.Exp, bias=sm[:, 8 + kk:9 + kk]), reads=[bgg, bsm], writes=[berow], n=128)
                S.op("act", lambda e, sl=sl, kk=kk: e.activation(out=bnd[:, sl], in_=cb[:, sl], func=AF.Exp, bias=sm[:, 8 + kk:9 + kk]), reads=[bcb, bsm], writes=[bbnd], n=128)
                S.op("dve", lambda e, sl=sl, kk=kk: e.scalar_tensor_tensor(out=junk[:], in0=erow[:, sl], scalar=1.0, in1=id32[:], op0=ALU.mult, op1=ALU.mult, accum_out=ecol[:, kk:kk + 1]),
                     reads=[berow, bconst], writes=[bjunk, becol], n=200)
            if DBG == 1:
                return
            for b in range(NBLK):
                for j in range(2):
                    S.op("pe", lambda e, b=b, j=j: e.transpose(out=pT[:, b * 256 + j * 128:b * 256 + (j + 1) * 128], in_=kh[:, j, b * 128:(b + 1) * 128], identity=idb[:]), reads=[bkh, bconst], writes=[bpT], n=128)
            S.op("act", lambda e: e.copy(out=ktok[:].rearrange("p b n -> p (b n)"), in_=pT[:]), reads=[bpT], writes=[bktok], n=700)
            for b in range(NBLK):
                for j in range(2):
                    S.op("pe", lambda e, b=b, j=j: e.transpose(out=pT[:, b * 256 + j * 128:b * 256 + (j + 1) * 128], in_=vf[:, j, b * 128:(b + 1) * 128], identity=idb[:]), reads=[bvf, bconst], writes=[bpT], n=128)
            for b in range(NBLK):
                S.op("dve", lambda e, b=b: e.tensor_scalar(out=vtok[:, b, 0:256], in0=pT[:, b * 256:(b + 1) * 256], scalar1=ecol[:, b:b + 1], scalar2=None, op0=ALU.mult), reads=[bpT, becol], writes=[bvtok], n=256)
                S.op("dve", lambda e, b=b: e.tensor_copy(out=vtok[:, b, 256:257], in_=ecol[:, b:b + 1]), reads=[becol], writes=[bvtok], n=1)
            if DBG == 2:
                return
            C_, bC_ = CT[l][hd], bCT[l][hd]
            S.default_n = 128
            qd, bqd = vf, bvf
            for b in range(NBLK):
                if ti_ == 0 and b == 0:
                    continue
                S.op("dve", lambda e, b=b: e.tensor_scalar(out=qd[:, :, b * 128:(b + 1) * 128], in0=qh[:, :, b * 128:(b + 1) * 128], scalar1=sm[:, 12 + b:13 + b], scalar2=None, op0=ALU.mult),
                     reads=[bqh, bsm, bvf], writes=[bqd], n=256)
            for b in range(NBLK):
                first = (ti_ == 0 and b == 0)
                sl = slice(b * 128, (b + 1) * 128)
                if b == 0 and not first:
                    S.op("act", lambda e: e.copy(out=Cbf[:, :, 0:257], in_=C_[:]), reads=[bC_], writes=[bCbf], n=514)
                for j in range(2):
                    S.op("pe", lambda e, j=j, sl=sl: e.matmul(pA[:, 0:128], lhsT=kh[:, j, sl], rhs=qh[:, j, sl], start=(j == 0), stop=(j == 1)), reads=[bkh, bqh], writes=[bpA])
                S.op("dve", lambda e: e.tensor_tensor(out=SmT[:], in0=pA[:, 0:128], in1=maskT[:], op=ALU.mult), reads=[bpA, bconst], writes=[bSmT])
                for vc in range(2):
                    osl = slice(vc * 128, (vc + 1) * 128)
                    if not first:
                        for j in range(2):
                            S.op("pe", lambda e, j=j, osl=osl, sl=sl: e.matmul(pB[:, osl], lhsT=Cbf[:, j, osl], rhs=qd[:, j, sl], start=(j == 0), stop=False), reads=[bCbf, bqd], writes=[bpB])
                    S.op("pe", lambda e, osl=osl, b=b, first=first: e.matmul(pB[:, osl], lhsT=vtok[:, b, osl], rhs=SmT[:], start=first, stop=True), reads=[bvtok, bSmT], writes=[bpB])
                if not first:
                    for j in range(2):
                        S.op("pe", lambda e, j=j, sl=sl: e.matmul(pB[:, 256:384], lhsT=Cbf[:, j, 256:257].to_broadcast([128, 128]), rhs=qd[:, j, sl], start=(j == 0), stop=False), reads=[bCbf, bqd], writes=[bpB])
                S.op("pe", lambda e, b=b, first=first: e.matmul(pB[:, 256:384], lhsT=vtok[:, b, 256:257].to_broadcast([128, 128]), rhs=SmT[:], start=first, stop=True), reads=[bvtok, bSmT], writes=[bpB])
                for j in range(2):
                    S.op("pe", lambda e, j=j, b=b: e.matmul(pSb[j][:, 0:257], lhsT=ktok[:, b, j * 128:(j + 1) * 128], rhs=vtok[:, b, 0:257], start=True, stop=True), reads=[bktok, bvtok], writes=[bpSb[j]], n=257)
                for j in range(2):
                    if first:
                        S.op("dve", lambda e, j=j: e.tensor_copy(out=C_[:, j, :], in_=pSb[j][:, 0:257]), reads=[bpSb[j]], writes=[bC_], n=257)
                    else:
                        S.op("dve", lambda e, j=j, b=b: e.scalar_tensor_tensor(out=C_[:, j, :], in0=C_[:, j, :], scalar=sm[:, 12 + b:13 + b], in1=pSb[j][:, 0:257], op0=ALU.mult, op1=ALU.add), reads=[bpSb[j], bC_, bsm], writes=[bC_], n=257)
                if b < NBLK - 1:
                    S.op("act", lambda e: e.copy(out=Cbf[:, :, 0:257], in_=C_[:]), reads=[bC_], writes=[bCbf], n=514)
                S.op("act", lambda e: e.activation(out=rd[:], in_=pB[:, 256:384], func=AF.Abs), reads=[bpB], writes=[brd])
                S.op("dve", lambda e, sl=sl: e.tensor_tensor(out=rd[:], in0=rd[:], in1=bnd[:, sl], op=ALU.max), reads=[brd, bbnd], writes=[brd])
                S.op("act", lambda e: e.activation(out=rd[:], in_=rd[:], func=AF.Ln), reads=[brd], writes=[brd], n=128)
                S.op("act", lambda e: e.activation(out=rd[:], in_=rd[:], func=AF.Exp, scale=-1.0), reads=[brd], writes=[brd], n=128)
                S.op("dve", lambda e, sl=sl: e.tensor_tensor(out=hr[:, :, sl], in0=pB[:, 0:256].rearrange("p (a n) -> p a n", a=2), in1=rd[:].unsqueeze(1).to_broadcast([128, 2, 128]), op=ALU.mult), reads=[bpB, brd], writes=[bhr], n=256)
            S.default_n = 512
            S.op("act", lambda e: e.activation(out=hsq[:], in_=hr[:], func=AF.Square), reads=[bhr], writes=[bhsq], n=1024)
            for vc in range(2):
                S.op("pe", lambda e, vc=vc: e.matmul(pA[:], lhsT=onesb[:], rhs=hsq[:, vc, :], start=(vc == 0), stop=(vc == 1)), reads=[bhsq, bconst], writes=[bpA])
            rsqrt_to(rs[:], brs, pA[:], bpA, 1.0 / 256.0)
            for vc in range(2):
                S.op("dve", lambda e, vc=vc: e.scalar_tensor_tensor(out=hm[:, 2 * hd + vc, :], in0=hr[:, vc, :], scalar=vec[l][:, V_MO + 2 * hd + vc:V_MO + 2 * hd + vc + 1], in1=rs[:], op0=ALU.mult, op1=ALU.mult),
                     reads=[bhr, brs, bvec[l]], writes=[bhm[2 * hd + vc]])

        def qknorm(l, pz, bpz, gcol, dst_ap, dst_buf):
            si = rr("sq", 2); pq = rr("pg", 3)
            S.op("act", lambda e: e.activation(out=sq[si][:], in_=pz[:], func=AF.Square), reads=[bpz], writes=[bsq[si]])
            S.op("pe", lambda e: e.matmul(pg_[pq][:], lhsT=blkb[:], rhs=sq[si][:], start=True, stop=True), reads=[bsq[si], bconst], writes=[bpg[pq]])
            rsqrt_to(rstd[:], brstd, pg_[pq][:], bpg[pq], 1.0 / 64.0)
            S.op("dve", lambda e: e.scalar_tensor_tensor(out=dst_ap, in0=pz[:], scalar=vec[l][:, gcol:gcol + 1], in1=rstd[:], op0=ALU.mult, op1=ALU.mult), reads=[bpz, brstd, bvec[l]], writes=[dst_buf])

        def swa_group(l, g, ti_):
            pp_ = g % 2
            aqh, baqh, avf, bavf = aqh_[pp_], baqh_[pp_], avf_[pp_], bavf_[pp_]
            pTs, bpTs, dtot, bdtot = pTs_[pp_], bpTs_[pp_], dtot_[pp_], bdtot_[pp_]
            for c in range(2):
                pz, bpz = gemm(l, B_IN + 40 + 2 * g + c, u, bu)
                qknorm(l, pz, bpz, V_AQ, aqh[:, c, :], baqh)
            pz, bpz = gemm(l, B_IN + 48 + g, u, bu)
            qknorm(l, pz, bpz, V_AK, akc[l][:, g, 128:128 + TT], bakc[l][g])
            pz, bpz = gemm(l, B_IN + 52 + g, u, bu)
            S.op("act", lambda e, pz=pz: e.copy(out=avf[:], in_=pz[:]), reads=[bpz], writes=[bavf])
            for b in range(NBLK):
                S.op("pe", lambda e, b=b: e.transpose(out=pT[:, b * 128:(b + 1) * 128], in_=avf[:, b * 128:(b + 1) * 128], identity=idb[:]), reads=[bavf, bconst], writes=[bpT], n=128)
            S.op("act", lambda e: e.copy(out=avt[l][:, g, 1:NBLK + 1, :].rearrange("p b n -> p (b n)"), in_=pT[:, 0:512]), reads=[bpT], writes=[bavt[l][g]])
            if DBG == 51:
                return
            for b in range(NBLK):
                first = (ti_ == 0 and b == 0)
                kbs = (1,) if first else (0, 1)
                qsl = slice(b * 128, (b + 1) * 128)
                for kb in kbs:
                    ksl = slice((b + kb) * 128, (b + kb + 1) * 128)
                    for j in range(4):
                        hf, a = j % 2, j // 2
                        col = kb * 256 + a * 128
                        S.op("pe", lambda e, hf=hf, a=a, col=col, ksl=ksl, qsl=qsl: e.matmul(pSb[hf][:, col:col + 128], lhsT=akc[l][hf * 64:(hf + 1) * 64, g, ksl], rhs=aqh[hf * 64:(hf + 1) * 64, a, qsl], start=True, stop=True),
                             reads=[bakc[l][g], baqh], writes=[bpSb[hf]], n=128)
                lo = 256 if first else 0
                if DBG == 52:
                    return
                for hf in range(2):
                    S.op("act", lambda e, lo=lo, hf=hf: e.activation(out=pTs[:, hf * 512 + lo:(hf + 1) * 512], in_=pSb[hf][:, lo:512], func=AF.Exp, scale=0.125), reads=[bpSb[hf]], writes=[bpTs])
                if DBG == 53:
                    continue
                pv = pTs[:].rearrange("p (hf r) -> p hf r", hf=2)
                S.op("dve", lambda e, lo=lo, pv=pv: e.tensor_tensor(out=pv[:, :, lo:512], in0=pv[:, :, lo:512], in1=swm[:].rearrange("p hf kb a q -> p hf (kb a q)")[:, :, lo:512], op=ALU.mult), reads=[bpTs, bconst], writes=[bpTs])
                if DBG == 54:
                    continue
                pk = pTs[:].rearrange("p (hf kb r) -> p hf kb r", hf=2, kb=2)
                for i_, kb in enumerate(kbs):
                    S.op("pe", lambda e, kb=kb, i_=i_, b=b, nk=len(kbs), pk=pk: e.matmul(pA[:], lhsT=avt[l][:, g, b + kb, :], rhs=pk[:, :, kb, :], start=(i_ == 0), stop=(i_ == nk - 1)), reads=[bavt[l][g], bpTs], writes=[bpA])
                for i_, kb in enumerate(kbs):
                    S.op("pe", lambda e, kb=kb, i_=i_, nk=len(kbs), pk=pk: e.matmul(pB[:], lhsT=onesb[:], rhs=pk[:, :, kb, :], start=(i_ == 0), stop=(i_ == nk - 1)), reads=[bconst, bpTs], writes=[bpB])
                if DBG == 55:
                    continue
                S.op("dve", lambda e: e.tensor_tensor(out=dtot[:].rearrange("p (hf a q) -> p hf a q", hf=2, a=2), in0=pB[:].rearrange("p (hf a q) -> p hf a q", hf=2, a=2),
                                                      in1=skx[l][:, 4 * g:4 * g + 4].rearrange("p (a hf) -> p hf a", hf=2).unsqueeze(3).to_broadcast([128, 2, 2, 128]), op=ALU.add), reads=[bpB, bvec[l]], writes=[bdtot])
                S.op("act", lambda e: e.activation(out=dtot[:], in_=dtot[:], func=AF.Ln), reads=[bdtot], writes=[bdtot])
                S.op("act", lambda e: e.activation(out=dtot[:], in_=dtot[:], func=AF.Exp, scale=-1.0), reads=[bdtot], writes=[bdtot])
                if DBG == 56:
                    continue
                for hf in range(2):
                    rows = slice(hf * 64, (hf + 1) * 64)
                    S.op("dve", lambda e, hf=hf, rows=rows, qsl=qsl: e.tensor_tensor(out=ha[rows, 2 * g:2 * g + 2, qsl],
                                                                                    in0=pA[rows, hf * 256:(hf + 1) * 256].rearrange("p (a q) -> p a q", a=2),
                                                                                    in1=dtot[rows, hf * 256:(hf + 1) * 256].rearrange("p (a q) -> p a q", a=2), op=ALU.mult),
                         reads=[bpA, bdtot], writes=[bha[2 * g], bha[2 * g + 1]], n=256)
            S.op("dve", lambda e: e.tensor_copy(out=akc[l][:, g, 0:128], in_=akc[l][:, g, TT:TT + 128]), reads=[bakc[l][g]], writes=[bakc[l][g]], n=128)
            S.op("dve", lambda e: e.tensor_copy(out=avt[l][:, g, 0, :], in_=avt[l][:, g, NBLK, :]), reads=[bavt[l][g]], writes=[bavt[l][g]], n=128)

        def mixer(l, ti_):
            for hd in range(4):
                mlstm_head(l, hd, ti_)
                if DBG in (1, 2, 3):
                    return
            if DBG == 4:
                return
            for c in range(8):
                pz, bpz = gemm(l, B_IN + 24 + c, u, bu)
                ti = rr("t2", 4)
                sigmoid_to(t2[ti][:], bt2[ti], pz[:], bpz)
                S.op("dve", lambda e, c=c, ti=ti: e.tensor_tensor(out=hm[:, c, :], in0=hm[:, c, :], in1=t2[ti][:], op=ALU.mult), reads=[bhm[c], bt2[ti]], writes=[bhm[c]])
            for g in range(4):
                swa_group(l, g, ti_)
                if DBG >= 5:
                    return
            if DBG == 6:
                return
            for d in range(8):
                pa_, bpa_ = gemm(l, B_A + d, hm, bhm)
                pgm, bpgm = gemm(l, B_IN + 56 + d, u, bu)
                ta = rr("t2", 4)
                sigmoid_to(t2[ta][:], bt2[ta], pgm[:], bpgm)
                S.op("dve", lambda e, ta=ta, pa_=pa_: e.tensor_tensor(out=t2[ta][:], in0=pa_[:], in1=t2[ta][:], op=ALU.mult), reads=[bpa_, bt2[ta]], writes=[bt2[ta]])
                pb_, bpb_ = gemm(l, B_B + d, ha, bha)
                pga, bpga = gemm(l, B_IN + 64 + d, u, bu)
                tb = rr("t2", 4)
                sigmoid_to(t2[tb][:], bt2[tb], pga[:], bpga)
                S.op("dve", lambda e, tb=tb, pb_=pb_: e.tensor_tensor(out=t2[tb][:], in0=pb_[:], in1=t2[tb][:], op=ALU.mult), reads=[bpb_, bt2[tb]], writes=[bt2[tb]])
                S.op("dve", lambda e, ta=ta, tb=tb, d=d: e.tensor_tensor(out=mg[:, d, :], in0=t2[ta][:], in1=t2[tb][:], op=ALU.add), reads=[bt2[ta], bt2[tb]], writes=[bmg[d]])
            for d in range(8):
                po, bpo = gemm(l, B_O + d, mg, bmg)
                S.op("dve", lambda e, d=d, po=po: e.tensor_tensor(out=h[:, d, :], in0=po[:], in1=h[:, d, :], op=ALU.add), reads=[bpo, bh[d]], writes=[bh[d]])

        def ple(l, t0):
            S.op("pool", lambda e: e.dma_start(out=ptb[:], in_=pTd[l, :, t0:t0 + TT].rearrange("(c p) t -> p c t", p=128)), writes=[bptb], dma="ptb")
            for d in range(8):
                pgt, bpgt = gemm(l, B_PG + d, u, bu)
                ti = rr("t2", 4)
                sigmoid_to(t2[ti][:], bt2[ti], pgt[:], bpgt)
                si = rr("wp", 2)
                wload("p", si, l, d, wpsl[si][:].rearrange("p k n -> p (k n)"), bwpsl[si], WP[l, d], WPs[l, d], f"wpsl{si}", f"wpst{si}", 65536)
                pi = rr("pg", 3)
                for k in range(2):
                    S.op("pe", lambda e, k=k, si=si, pi=pi: e.matmul(pg_[pi][:], lhsT=wpsl[si][:, k, :], rhs=ptb[:, k, :], start=(k == 0), stop=(k == 1)), reads=[bwpsl[si], bptb], writes=[bpg[pi]])
                S.op("dve", lambda e, ti=ti, pi=pi: e.tensor_tensor(out=t2[ti][:], in0=pg_[pi][:], in1=t2[ti][:], op=ALU.mult), reads=[bpg[pi], bt2[ti]], writes=[bt2[ti]])
                S.op("dve", lambda e, ti=ti, d=d: e.tensor_tensor(out=h[:, d, :], in0=h[:, d, :], in1=t2[ti][:], op=ALU.add), reads=[bh[d], bt2[ti]], writes=[bh[d]])

        for ti_ in range(NT):
            t0 = ti_ * TT
            state["tile"] = ti_
            S.op("sp", lambda e, t0=t0: e.dma_start(out=h[:], in_=xT[:, t0:t0 + TT].rearrange("(c p) t -> p c t", p=128)), writes=bh, dma="hld")
            for l in range(L):
                norm(l, V_N1)
                ffn(l, B_G1, B_U1, 0)
                if debug_stop == "ffn1":
                    break
                norm(l, V_NM)
                mixer(l, ti_)
                if debug_stop == "mix":
                    break
                norm(l, V_N2)
                ffn(l, B_G2, B_U2, 8)
                norm(l, V_NP)
                ple(l, t0)
            S.op("sp", lambda e, t0=t0: e.dma_start(out=outT[:, t0:t0 + TT].rearrange("(c p) t -> p c t", p=128), in_=h[:]), reads=bh, dma="hst")

        sems = {e: es.enter_context(nc.semaphore("s_" + e)) for e in Sched.ENGS}
        dsem = {k: es.enter_context(nc.semaphore("d_" + k)) for k in dma_keys if k in S.dma_counts}
        S.emit(sems, dsem, final_waits=["hst"])
    return nc


def _prep_inputs(inputs, L):
    per = [_prep_layer(l, inputs) for l in range(L)]
    wk = np.stack([p[0] for p in per]); wd = np.stack([p[1] for p in per])
    wp = np.stack([p[2] for p in per]); vec = np.stack([p[3] for p in per])
    return wk, wd, wp, vec


def kernel(**inputs):
    x = np.asarray(inputs["x"], np.float32)
    p = np.asarray(inputs["p"], np.float32)
    B, T, _ = x.shape
    L = p.shape[0]
    wk, wd, wp, vec = _prep_inputs(inputs, L)
    nc = build_program(T, L)
    in_maps = []
    for b in range(B):
        in_maps.append({"xT": np.ascontiguousarray(x[b].T), "pTin": np.ascontiguousarray(p[:, b].transpose(0, 2, 1)),
                        "wk": wk, "wd": wd, "wp": wp, "vec": vec})
    res = run_bass_kernel_spmd(nc, in_maps, core_ids=list(range(B)))
    out = np.stack([np.ascontiguousarray(res.results[b]["outT"].T) for b in range(B)])
    return out.astype(np.float32)


_BP_LINE = build_program.__code__.co_firstlineno
```
